# Optimizing a Trainium2 kernel written in Bass

```python
import math
import jax, jax.numpy as jnp
from jax import lax
import numpy as np

D_MODEL = 4096
BATCH = 4
SEQ = 4096
DEPTH = 1

CTX_LEN = 256
GRID_W = 64
A_WIDTH = 2048
A_HEADS = 16
A_HEAD_K = 128
A_HEAD_V = A_WIDTH // A_HEADS
A_FDIM = A_HEADS * A_HEAD_K
SCAN_CHUNK = 64
B_WIDTH = 2048
B_GROUPS = 16
B_GROUP_CH = B_WIDTH // B_GROUPS
MIX_CHUNK = 128
D_FF = 4 * D_MODEL
N_MOD = 6
ALPHA = (2.0 * DEPTH) ** 0.25
BETA = (8.0 * DEPTH) ** -0.25
LN_EPS = 1e-6
POS_BASE = 10000.0
IN_SIZES = (A_FDIM, A_WIDTH, A_FDIM, A_FDIM, A_WIDTH, B_WIDTH, B_WIDTH, D_MODEL, D_MODEL)
N_IN = sum(IN_SIZES)

kernel_name = "hybrid_hgrn2_chunkmlp_dit_block"


def _layernorm(x, g=None, b=None):
    xf = x.astype(jnp.float32)
    mu = jnp.mean(xf, axis=-1, keepdims=True)
    xc = xf - mu
    y = xc * lax.rsqrt(jnp.mean(xc * xc, axis=-1, keepdims=True) + LN_EPS)
    if g is not None:
        y = y * g.astype(jnp.float32) + b.astype(jnp.float32)
    return y.astype(x.dtype)


def _adaln(x, shift, scale):
    return _layernorm(x) * (1.0 + scale) + shift


def _sincos_2d(rows, cols, dim):
    quarter = dim // 4
    omega = 1.0 / (POS_BASE ** (jnp.arange(quarter, dtype=jnp.float32) / quarter))
    r, cc = jnp.meshgrid(jnp.arange(rows, dtype=jnp.float32), jnp.arange(cols, dtype=jnp.float32), indexing="ij")
    def emb(p):
        a = p.reshape(-1)[:, None] * omega[None, :]
        return jnp.concatenate([jnp.sin(a), jnp.cos(a)], axis=-1)
    return jnp.concatenate([emb(r), emb(cc)], axis=-1)


def _in_proj(h, w_in):
    idx, acc = [], 0
    for s in IN_SIZES[:-1]:
        acc += s
        idx.append(acc)
    return jnp.split(h @ w_in, idx, axis=-1)


def _rec_inputs(q, i, zf, zb, lb_f, lb_b):
    bn, t, _ = q.shape
    def heads(a, d):
        return a.astype(jnp.float32).reshape(bn, t, A_HEADS, d)
    qh = heads(jax.nn.silu(q), A_HEAD_K) * (A_HEAD_K ** -0.5)
    vh = heads(i, A_HEAD_V)
    def gate(z, lb):
        z = z.astype(jnp.float32)
        logf = jnp.log(lb + (1.0 - lb) * jax.nn.sigmoid(z))
        k = (1.0 - lb) * jax.nn.sigmoid(-z)
        return heads(k, A_HEAD_K), heads(logf, A_HEAD_K)
    kf, lff = gate(zf, lb_f)
    kb, lfb = gate(zb, lb_b)
    return qh, vh, kf, lff, kb, lfb


def _chunk_scan(q, k, v, logf, s0):
    bn, t, h, _ = q.shape
    n = t // SCAN_CHUNK
    def to_chunks(a):
        return a.reshape(bn, n, SCAN_CHUNK, h, a.shape[-1]).transpose(1, 0, 3, 2, 4)
    mask = jnp.tril(jnp.ones((SCAN_CHUNK, SCAN_CHUNK), dtype=bool))
    def step(s, inp):
        qc, kc, vc, lc = inp
        b = jnp.cumsum(lc, axis=2)
        diff = b[:, :, :, None, :] - b[:, :, None, :, :]
        decay = jnp.exp(jnp.where(mask[:, :, None], diff, -jnp.inf))
        att = jnp.einsum("bhtsk,bhsk->bhts", qc[:, :, :, None, :] * decay, kc)
        o = jnp.einsum("bhts,bhsv->bhtv", att, vc) + jnp.einsum("bhtk,bhkv->bhtv", qc * jnp.exp(b), s)
        b_last = b[:, :, -1:, :]
        s_new = jnp.exp(b_last[:, :, 0, :])[..., None] * s + jnp.einsum("bhsk,bhsv->bhkv", kc * jnp.exp(b_last - b), vc)
        return s_new, o
    s_fin, o = lax.scan(step, s0, (to_chunks(q), to_chunks(k), to_chunks(v), to_chunks(logf)))
    o = o.transpose(1, 0, 3, 2, 4).reshape(bn, t, h, v.shape[-1])
    return s_fin, o


def _bidir(qh, vh, kf, lff, kb, lfb, s_f, s_b):
    flip = lambda a: jnp.flip(a, axis=1)
    sf, of = _chunk_scan(qh, kf, vh, lff, s_f)
    sb, ob = _chunk_scan(flip(qh), flip(kb), flip(vh), flip(lfb), s_b)
    return sf, sb, of + flip(ob)


def _hgrn2_readout(o, g, gain):
    bn, t = o.shape[0], o.shape[1]
    o = o * lax.rsqrt(jnp.mean(o * o, axis=-1, keepdims=True) + LN_EPS) * gain.astype(jnp.float32)
    return (o.reshape(bn, t, A_WIDTH) * jax.nn.silu(g.astype(jnp.float32))).astype(g.dtype)


def _chunk_mix(u, v, v_g, v_b, w_s, b_s):
    bn, t, _ = v.shape
    n = t // MIX_CHUNK
    vn = _layernorm(v, v_g, v_b).reshape(bn, n, MIX_CHUNK, B_GROUPS, B_GROUP_CH)
    mixed = jnp.einsum("gts,bnsgc->bntgc", w_s, vn) + b_s.T[None, None, :, :, None]
    return u * mixed.reshape(bn, t, B_WIDTH)


def _stream_update(x, p, o_rec, mod, a_norm_g, v_norm_g, v_norm_b, w_s, b_s, w_proj_a, w_proj_b,
                   w_out, ln1_g, ln1_b, w_ff1, w_ff2, ln2_g, ln2_b):
    y_a = _hgrn2_readout(o_rec, p[4], a_norm_g)
    y_b = _chunk_mix(p[5], p[6], v_norm_g, v_norm_b, w_s, b_s)
    merged = jax.nn.sigmoid(p[7]) * (y_a @ w_proj_a) + jax.nn.sigmoid(p[8]) * (y_b @ w_proj_b)
    x = _layernorm(ALPHA * x + mod[2] * (merged @ w_out), ln1_g, ln1_b)
    h = _adaln(x, mod[3], mod[4])
    ff = jnp.square(jax.nn.relu(h @ w_ff1)) @ w_ff2
    return _layernorm(ALPHA * x + mod[5] * ff, ln2_g, ln2_b)


def setup_inputs(seed: int = 0) -> dict:
    key = jax.random.key(seed)
    ks = jax.random.split(key, 24)
    f32 = jnp.float32
    nrm = lambda k, shape, s: jax.random.normal(k, shape, f32) * s
    return {
        "x": nrm(ks[0], (BATCH, SEQ, D_MODEL), 1.0),
        "c": nrm(ks[1], (BATCH, D_MODEL), 1.0),
        "ctx": nrm(ks[2], (BATCH, CTX_LEN, D_MODEL), 1.0),
        "c_ctx": nrm(ks[3], (D_MODEL,), 1.0),
        "w_ada": nrm(ks[4], (DEPTH, D_MODEL, N_MOD * D_MODEL), 0.5 * D_MODEL ** -0.5),
        "b_ada": nrm(ks[5], (DEPTH, N_MOD * D_MODEL), 0.02),
        "w_in": nrm(ks[6], (DEPTH, D_MODEL, N_IN), D_MODEL ** -0.5),
        "lb_logits": nrm(ks[7], (2, DEPTH + 1, A_FDIM), 0.5),
        "a_norm_g": 1.0 + nrm(ks[8], (DEPTH, A_HEAD_V), 0.02),
        "w_proj_a": nrm(ks[9], (DEPTH, A_WIDTH, D_MODEL), BETA * A_WIDTH ** -0.5),
        "v_norm_g": 1.0 + nrm(ks[10], (DEPTH, B_WIDTH), 0.02),
        "v_norm_b": nrm(ks[11], (DEPTH, B_WIDTH), 0.02),
        "w_s": nrm(ks[12], (DEPTH, B_GROUPS, MIX_CHUNK, MIX_CHUNK), MIX_CHUNK ** -0.5),
        "b_s": 1.0 + nrm(ks[13], (DEPTH, B_GROUPS, MIX_CHUNK), 0.02),
        "w_proj_b": nrm(ks[14], (DEPTH, B_WIDTH, D_MODEL), BETA * B_WIDTH ** -0.5),
        "w_out": nrm(ks[15], (DEPTH, D_MODEL, D_MODEL), BETA * D_MODEL ** -0.5),
        "ln1_g": 1.0 + nrm(ks[16], (DEPTH, D_MODEL), 0.02),
        "ln1_b": nrm(ks[17], (DEPTH, D_MODEL), 0.02),
        "w_ff1": nrm(ks[18], (DEPTH, D_MODEL, D_FF), BETA * D_MODEL ** -0.5),
        "w_ff2": nrm(ks[19], (DEPTH, D_FF, D_MODEL), BETA * D_FF ** -0.5),
        "ln2_g": 1.0 + nrm(ks[20], (DEPTH, D_MODEL), 0.02),
        "ln2_b": nrm(ks[21], (DEPTH, D_MODEL), 0.02),
    }


def reference(x, c, ctx, c_ctx, w_ada, b_ada, w_in, lb_logits, a_norm_g, w_proj_a, v_norm_g, v_norm_b,
              w_s, b_s, w_proj_b, w_out, ln1_g, ln1_b, w_ff1, w_ff2, ln2_g, ln2_b):
    bn, t, _ = x.shape
    rows = t // GRID_W
    x = x + _sincos_2d(rows, GRID_W, D_MODEL).astype(x.dtype)[None]
    xc = ctx
    lb_all = jnp.cumsum(jax.nn.softmax(lb_logits.astype(jnp.float32), axis=1), axis=1)
    zero_state = jnp.zeros((bn, A_HEADS, A_HEAD_K, A_HEAD_V), jnp.float32)
    for l in range(DEPTH):
        last = l == DEPTH - 1
        mx = jnp.split((jax.nn.silu(c) @ w_ada[l] + b_ada[l])[:, None, :], N_MOD, axis=-1)
        mc = jnp.split((jax.nn.silu(c_ctx) @ w_ada[l] + b_ada[l])[None, None, :], N_MOD, axis=-1)
        lb_f, lb_b = lb_all[0, l], lb_all[1, l]
        px = _in_proj(_adaln(x, mx[0], mx[1]), w_in[l])
        pc = _in_proj(_adaln(xc, mc[0], mc[1]), w_in[l])
        s_f, s_b, o_c = _bidir(*_rec_inputs(pc[0], pc[1], pc[2], pc[3], lb_f, lb_b), zero_state, zero_state)
        _, _, o_x = _bidir(*_rec_inputs(px[0], px[1], px[2], px[3], lb_f, lb_b), s_f, s_b)
        lw = (a_norm_g[l], v_norm_g[l], v_norm_b[l], w_s[l], b_s[l], w_proj_a[l], w_proj_b[l], w_out[l],
              ln1_g[l], ln1_b[l], w_ff1[l], w_ff2[l], ln2_g[l], ln2_b[l])
        x_new = _stream_update(x, px, o_x, mx, *lw)
        if not last:
            xc = _stream_update(xc, pc, o_c, mc, *lw)
        x = x_new
    return x
```

```python
import math
from contextlib import ExitStack
import numpy as np
import concourse.bass as bass
import concourse.mybir as mybir
from concourse.bass_utils import run_bass_kernel_spmd

F32 = mybir.dt.float32
BF16 = mybir.dt.bfloat16
I32 = mybir.dt.int32
AF = mybir.ActivationFunctionType
ALU = mybir.AluOpType

LN_EPS = 1e-6
SAME_ENGINE_SYNC = True
SAME_ENGINE_WAR = False
GRID_W = 64
P = 128


class Cfg:
    def __init__(self, D=4096, SEQ=4096, BATCH=4, CTX=256, H=16, G=16, DFF=16384):
        self.D, self.SEQ, self.BATCH, self.CTX, self.H, self.G, self.DFF = D, SEQ, BATCH, CTX, H, G, DFF
        self.AW = H * 128
        self.BW = G * 128
        self.NT = SEQ // 2
        self.nt = self.NT // 128
        self.nc_ = CTX // 128
        self.DC = D // 128
        self.NMOD = 6
        self.ROWS = SEQ // GRID_W
        self.ALPHA = 2.0 ** 0.25
        sizes = (self.AW, self.AW, self.AW, self.AW, self.AW, self.BW, self.BW, D, D)
        offs = np.cumsum((0,) + sizes)
        self.off = dict(zip(("q", "i", "zf", "zb", "g", "u", "v", "ga", "gb"), offs[:-1]))
        self.NIN = int(offs[-1])


class Buf:
    __slots__ = ("w", "r")

    def __init__(self):
        self.w = None
        self.r = {}


class Sched:
    def __init__(self, nc, stack, ndma=6):
        self.nc = nc
        self.eng = {"pe": nc.tensor, "act": nc.scalar, "dve": nc.vector, "pool": nc.gpsimd, "sp": nc.sync}
        self.csem = {e: stack.enter_context(nc.semaphore("c_" + e)) for e in ("pe", "act", "dve", "pool")}
        self.cnt = {e: 0 for e in self.csem}
        self.seen = {e: {} for e in self.eng}
        self.dsem = {q: [stack.enter_context(nc.semaphore("d_%s%d" % (q, i))) for i in range(ndma)]
                     for q in ("sp", "pool")}
        self.duse = {q: [0] * ndma for q in self.dsem}
        self.dnext = {q: 0 for q in self.dsem}

    def _waits(self, eng, deps):
        need = {}
        for d in deps:
            if d is None:
                continue
            if len(d) == 4:
                if d[1] == eng and d[0] == "c" and not SAME_ENGINE_WAR:
                    continue
                d = d[:3]
            kind, key, val = d
            if kind == "c" and key == eng and (eng == "pe" or not SAME_ENGINE_SYNC):
                continue
            k = (kind, key)
            if val > need.get(k, 0):
                need[k] = val
        e = self.eng[eng]
        seen = self.seen[eng]
        for k, val in need.items():
            if seen.get(k, 0) >= val:
                continue
            seen[k] = val
            sem = self.csem[k[1]] if k[0] == "c" else k[1]
            e.wait_ge(sem, val)

    @staticmethod
    def _deps(reads, writes):
        deps = []
        for b in reads:
            deps.append(b.w)
        for b in writes:
            deps.append(b.w)
            deps.extend(t + ("war",) for t in b.r.values())
        return deps

    @staticmethod
    def _mark(tok, reads, writes):
        for b in writes:
            b.w = tok
            b.r = {}
        for b in reads:
            b.r[(tok[0], tok[1])] = tok

    def op(self, eng, fn, reads=(), writes=()):
        self._waits(eng, self._deps(reads, writes))
        ins = fn(self.eng[eng])
        self.cnt[eng] += 1
        ins.then_inc(self.csem[eng], 1)
        self._mark(("c", eng, self.cnt[eng]), reads, writes)

    def dma(self, q, out, in_, reads=(), writes=()):
        i = self.dnext[q]
        self.dnext[q] = (i + 1) % len(self.dsem[q])
        sem = self.dsem[q][i]
        deps = self._deps(reads, writes)
        if self.duse[q][i]:
            deps.append(("d", sem, 16 * self.duse[q][i]))
        self._waits(q, deps)
        self.eng[q].dma_start(out=out, in_=in_).then_inc(sem, 16)
        self.duse[q][i] += 1
        self._mark(("d", sem, 16 * self.duse[q][i]), reads, writes)

    def barrier(self):
        toks = [("c", e, self.cnt[e]) for e in self.csem if self.cnt[e]]
        for q in self.dsem:
            for sem, n in zip(self.dsem[q], self.duse[q]):
                if n:
                    toks.append(("d", sem, 16 * n))
        for e in self.eng:
            self._waits_all(e, toks)

    def _waits_all(self, eng, toks):
        e = self.eng[eng]
        seen = self.seen[eng]
        for kind, key, val in toks:
            k = (kind, key)
            if seen.get(k, 0) >= val:
                continue
            seen[k] = val
            e.wait_ge(self.csem[key] if kind == "c" else key, val)


def bcast_rows(ap1d, nrows, n):
    return bass.AP(ap1d.tensor, ap1d.offset, [[0, nrows], [1, n]])


class Builder:
    def __init__(self, cfg, debug_outs=()):
        self.cfg = cfg
        self.debug_outs = debug_outs
        self.nc = bass.Bass("TRN2", target_bir_lowering=False)
        self.stack = ExitStack()
        self.S = None

    def din(self, name, shape, dt=F32):
        return self.nc.dram_tensor(name, list(shape), dt, kind="ExternalInput").ap()

    def dscr(self, name, shape, dt):
        kind = "ExternalOutput" if name in self.debug_outs else "Internal"
        return self.nc.dram_tensor(name, list(shape), dt, kind=kind).ap()

    def _uniq(self, name):
        self._n = getattr(self, "_n", 0) + 1
        return "%s_%d" % (name, self._n)

    def sb(self, st, name, shape, dt):
        return st.enter_context(self.nc.sbuf_tensor(self._uniq(name), list(shape), dt))

    def ps(self, st, name, shape, dt=F32):
        return st.enter_context(self.nc.psum_tensor(self._uniq(name), list(shape), dt))

    def build(self):
        cfg, nc = self.cfg, self.nc
        D, NT, CTX, DC = cfg.D, cfg.NT, cfg.CTX, cfg.DC
        with self.stack as stack:
            S = self.S = Sched(nc, stack)
            io = self.io = {}
            io["xloc"] = self.din("xloc", [2 * NT, D])
            io["ctxl"] = self.din("ctxl", [CTX, D])
            io["cvec"] = self.din("cvec", [2, D])
            io["w_ada"] = self.din("w_ada", [D, 6 * D])
            io["b_ada"] = self.din("b_ada", [6 * D])
            io["w_in"] = self.din("w_in", [D, cfg.NIN])
            io["w_zP"] = self.din("w_zP", [D, cfg.AW])
            io["w_zR"] = self.din("w_zR", [D, cfg.AW])
            io["lblP"] = self.din("lblP", [2, cfg.AW])
            io["lblR"] = self.din("lblR", [2, cfg.AW])
            io["a_norm_g"] = self.din("a_norm_g", [128])
            io["w_proj_a"] = self.din("w_proj_a", [cfg.AW, D])
            io["v_norm_g"] = self.din("v_norm_g", [cfg.BW])
            io["v_norm_b"] = self.din("v_norm_b", [cfg.BW])
            io["w_sT"] = self.din("w_sT", [cfg.G, 128, 128])
            io["b_s"] = self.din("b_s", [cfg.G, 128])
            io["w_proj_b"] = self.din("w_proj_b", [cfg.BW, D])
            io["w_out"] = self.din("w_out", [D, D])
            io["ln1_g"] = self.din("ln1_g", [D])
            io["ln1_b"] = self.din("ln1_b", [D])
            io["w_ff1"] = self.din("w_ff1", [D, cfg.DFF])
            io["w_ff2"] = self.din("w_ff2", [cfg.DFF, D])
            io["ln2_g"] = self.din("ln2_g", [D])
            io["ln2_b"] = self.din("ln2_b", [D])
            io["rowsel"] = self.din("rowsel", [2 * cfg.nt, 64, 128])
            io["colsel"] = self.din("colsel", [64, 128])
            io["out"] = nc.dram_tensor("out", [NT, D], F32, kind="ExternalOutput").ap()
            self.body(stack)
            S.barrier()
        return nc


    def gemm_f(self, *a, **k):
        for _ in self.gemm_f_gen(*a, **k):
            pass

    def gemm_f_gen(self, st, name, xt_sb, xbuf, KC, T, chunks, epi, nsets=2, pre=None):
        S, nc = self.S, self.nc
        KG = min(KC, 32)
        ngrp = KC // KG
        NB = min(T, 512)
        nb = T // NB
        nslot = 3
        wsl = [self.sb(st, "%s_w%d" % (name, i), [128, KG, 128], BF16) for i in range(nslot)]
        wbuf = [Buf() for _ in range(nslot)]
        banks = [[self.ps(st, "%s_p%d_%d" % (name, s, b), [128, 512], F32) for b in range(nb)] for s in range(nsets)]
        bbuf = [[Buf() for _ in range(nb)] for _ in range(nsets)]
        wi = 0
        for idx, (w_ap, tag) in enumerate(chunks):
            s = idx % nsets
            if pre is not None:
                pre(idx, tag)
            for g in range(ngrp):
                sl = wi % nslot
                wi += 1
                if isinstance(w_ap, tuple):
                    S.dma("sp", wsl[sl][:], w_ap[1][:, g * KG * 128:(g + 1) * KG * 128].rearrange("p (c f) -> p c f", f=128),
                          reads=[w_ap[2]], writes=[wbuf[sl]])
                else:
                    src = w_ap[g * KG * 128:(g + 1) * KG * 128, :].rearrange("(c p) f -> p c f", p=128)
                    S.dma("pool", wsl[sl][:], src, writes=[wbuf[sl]])
                for kc in range(KG):
                    first = (g == 0 and kc == 0)
                    last = (g == ngrp - 1 and kc == KG - 1)
                    for b in range(nb):
                        S.op("pe", lambda e, s=s, b=b, sl=sl, kc=kc, g=g, first=first, last=last: e.matmul(
                            banks[s][b][:, 0:NB], lhsT=wsl[sl][:, kc, :],
                            rhs=xt_sb[:, g * KG + kc, b * NB:(b + 1) * NB], start=first, stop=last),
                            reads=[wbuf[sl], xbuf], writes=[bbuf[s][b]])
            epi(idx, tag, [(banks[s][b], bbuf[s][b]) for b in range(nb)], NB)
            yield idx

    def load_xt(self, st, name, src_ap, KC, T):
        S = self.S
        xt = self.sb(st, name, [128, KC, T], BF16)
        xb = Buf()
        step = max(1, KC // 4)
        for c0 in range(0, KC, step):
            S.dma("sp", xt[:, c0:c0 + step, :],
                  src_ap[c0 * 128:(c0 + step) * 128, :].rearrange("(c p) t -> p c t", p=128), writes=[xb])
        return xt, xb

    def load_fm(self, st, name, vec_ap, n, dst, dbuf, c0=0):
        S = self.S
        R = n // 128
        if not hasattr(self, "_fm") or self._fm[0] is not st:
            self._fm = (st, [self.sb(st, "fm_t%d" % i, [128, 128], F32) for i in range(2)], [Buf(), Buf()],
                        [self.ps(st, "fm_p%d" % i, [128, 128], F32) for i in range(2)], [Buf(), Buf()], [0])
        _, tmps, tbs, pts, pbs, k = self._fm
        for r0 in range(0, R, 128):
            rr = min(128, R - r0)
            i = k[0] % 2
            k[0] += 1
            tmp, tb, pt, pb = tmps[i], tbs[i], pts[i], pbs[i]
            S.dma("sp", tmp[0:rr, :], vec_ap[r0 * 128:(r0 + rr) * 128].rearrange("(r c) -> r c", c=128), writes=[tb])
            S.op("pe", lambda e, rr=rr, tmp=tmp, pt=pt: e.transpose(pt[:, 0:rr], tmp[0:rr, :], self.ident_f[0:rr, 0:rr]),
                 reads=[tb, self.cbuf], writes=[pb])
            S.op("dve", lambda e, rr=rr, pt=pt, r0=r0: e.tensor_copy(dst[:, c0 + r0:c0 + r0 + rr], pt[:, 0:rr]),
                 reads=[pb], writes=[dbuf])

    def body(self, stack):
        cfg, nc, S, io = self.cfg, self.nc, self.S, self.io
        D, NT, CTX, DC, H, G = cfg.D, cfg.NT, cfg.CTX, cfg.DC, cfg.H, cfg.G
        AW, BW, DFF = cfg.AW, cfg.BW, cfg.DFF
        nt, nct = cfg.nt, cfg.nc_
        NX = 2 * NT
        d = self.d = {}
        d["hTx"] = self.dscr("hTx", [D, NX], BF16)
        d["hTc"] = self.dscr("hTc", [D, CTX], BF16)
        d["qsT"] = self.dscr("qsT", [AW, NT], BF16)
        d["gsT"] = self.dscr("gsT", [AW, NT], BF16)
        d["iT"] = self.dscr("iT", [AW, CTX + NX], BF16)
        d["lfP"] = self.dscr("lfP", [AW, CTX + NT], F32)
        d["kP"] = self.dscr("kP", [AW, CTX + NT], BF16)
        d["lfR"] = self.dscr("lfR", [AW, CTX + NX], F32)
        d["kR"] = self.dscr("kR", [AW, CTX + NX], BF16)
        d["uT"] = self.dscr("uT", [BW, NT], BF16)
        d["vT"] = self.dscr("vT", [BW, NT], F32)
        d["sgaT"] = self.dscr("sgaT", [D, NT], BF16)
        d["sgbT"] = self.dscr("sgbT", [D, NT], BF16)
        d["yaT"] = self.dscr("yaT", [AW, NT], BF16)
        d["ybT"] = self.dscr("ybT", [BW, NT], BF16)
        d["mA"] = self.dscr("mA", [D, NT], F32)
        d["mergedT"] = self.dscr("mergedT", [D, NT], BF16)
        d["x1pre"] = self.dscr("x1pre", [NT, D], F32)
        d["x1"] = self.dscr("x1", [NT, D], F32)
        d["h2T"] = self.dscr("h2T", [D, NT], BF16)
        d["hidT"] = self.dscr("hidT", [DFF, NT], BF16)
        d["x2pre"] = self.dscr("x2pre", [NT, D], F32)
        d["w2t"] = self.dscr("w2t", [D, DFF], BF16)
        self.w2tb = Buf()

        self.cbuf = Buf()
        self.ident_f = self.sb(stack, "ident_f", [128, 128], F32)
        self.ident_b = self.sb(stack, "ident_b", [128, 128], BF16)
        self.maskP = self.sb(stack, "maskP", [128, 128], I32)
        self.maskR = self.sb(stack, "maskR", [128, 128], I32)
        cin = self.din("consts", [3, 128, 128])
        ctmp = self.sb(stack, "ctmp", [128, 2, 128], F32)
        S.dma("sp", self.ident_f[:], cin[0], writes=[self.cbuf])
        S.dma("sp", ctmp[:, 0, :], cin[1], writes=[self.cbuf])
        S.dma("sp", ctmp[:, 1, :], cin[2], writes=[self.cbuf])
        S.op("dve", lambda e: e.tensor_copy(self.ident_b[:], self.ident_f[:]), reads=[self.cbuf], writes=[self.cbuf])
        S.op("dve", lambda e: e.tensor_copy(self.maskP[:], ctmp[:, 0, :]), reads=[self.cbuf], writes=[self.cbuf])
        S.op("dve", lambda e: e.tensor_copy(self.maskR[:], ctmp[:, 1, :]), reads=[self.cbuf], writes=[self.cbuf])
        self.modx = self.sb(stack, "modx", [128, 6 * DC], F32)
        self.modc = self.sb(stack, "modc", [128, 6 * DC], F32)
        self.mbuf = Buf()
        self.lb = self.sb(stack, "lb", [128, 4, H], F32)
        self.gain = self.sb(stack, "gain", [128, 1], F32)
        self.Etab = self.sb(stack, "Etab", [64, D // 2], F32)
        self.ebuf = Buf()

        self.phase_consts()
        S.barrier()
        self.phase_inproj()
        S.barrier()
        self.phase_scan()
        S.barrier()
        self.phase_mixb()
        S.barrier()
        self.phase_merge()
        S.barrier()
        self.phase_wout()
        S.barrier()
        self.phase_rows1()
        S.barrier()
        self.phase_ff()
        S.barrier()
        self.phase_rows2()

    def phase_consts(self):
        cfg, nc, S, io = self.cfg, self.nc, self.S, self.io
        D, DC, H, AW = cfg.D, cfg.DC, cfg.H, cfg.AW
        Q = D // 4
        with ExitStack() as st:
            jf = self.sb(st, "jf", [64, Q], F32)
            rf = self.sb(st, "rf", [64, 1], F32)
            ang = self.sb(st, "ang", [64, Q], F32)
            t1 = self.sb(st, "t1", [64, Q], F32)
            ki = self.sb(st, "ki", [64, Q], I32)
            b1 = Buf()
            S.op("pool", lambda e: e.iota(jf[:], [[1, Q]], base=0, channel_multiplier=0,
                                          allow_small_or_imprecise_dtypes=True), writes=[b1])
            S.op("pool", lambda e: e.iota(rf[:], [[0, 1]], base=0, channel_multiplier=1,
                                          allow_small_or_imprecise_dtypes=True), writes=[b1])
            S.op("act", lambda e: e.activation(out=ang[:], in_=jf[:], func=AF.Exp, scale=-math.log(10000.0) / Q),
                 reads=[b1], writes=[b1])
            S.op("dve", lambda e: e.tensor_scalar(ang[:], ang[:], rf[:, 0:1], None, op0=ALU.mult), reads=[b1], writes=[b1])
            S.op("dve", lambda e: e.tensor_scalar(t1[:], ang[:], 1.0 / (2 * math.pi), None, op0=ALU.mult), reads=[b1], writes=[b1])
            S.op("dve", lambda e: e.tensor_copy(ki[:], t1[:]), reads=[b1], writes=[b1])
            S.op("dve", lambda e: e.tensor_copy(t1[:], ki[:]), reads=[b1], writes=[b1])
            S.op("dve", lambda e: e.scalar_tensor_tensor(out=ang[:], in0=t1[:], scalar=-2 * math.pi, in1=ang[:],
                                                         op0=ALU.mult, op1=ALU.add), reads=[b1], writes=[b1])
            t2 = self.sb(st, "t2w", [64, Q], F32)

            def wrap(shift, extra=()):
                S.op("dve", lambda e: e.tensor_scalar(t1[:], ang[:], shift, None, op0=ALU.add), reads=[b1] + list(extra), writes=[b1])
                S.op("dve", lambda e: e.tensor_scalar(t2[:], t1[:], math.pi, -2 * math.pi, op0=ALU.is_gt, op1=ALU.mult), reads=[b1], writes=[b1])
                S.op("dve", lambda e: e.tensor_tensor(out=t1[:], in0=t1[:], in1=t2[:], op=ALU.add), reads=[b1], writes=[b1])
                S.op("dve", lambda e: e.tensor_scalar(t2[:], t1[:], -math.pi, 2 * math.pi, op0=ALU.is_lt, op1=ALU.mult), reads=[b1], writes=[b1])
                S.op("dve", lambda e: e.tensor_tensor(out=t1[:], in0=t1[:], in1=t2[:], op=ALU.add), reads=[b1], writes=[b1])
            wrap(0.0)
            S.op("act", lambda e: e.activation(out=self.Etab[:, 0:Q], in_=t1[:], func=AF.Sin), reads=[b1], writes=[self.ebuf])
            wrap(math.pi / 2, [self.ebuf])
            S.op("act", lambda e: e.activation(out=self.Etab[:, Q:2 * Q], in_=t1[:], func=AF.Sin), reads=[b1], writes=[self.ebuf])

            ltmp = self.sb(st, "ltmp", [128, 4, H], F32)
            lbuf = Buf()
            for i, nm in enumerate(("lblP", "lblR")):
                for r in range(2):
                    self.load_fm(st, "lb%d%d" % (i, r), io[nm][r], AW, ltmp[:, 2 * i + r, :], lbuf)
            for i in range(2):
                S.op("dve", lambda e, i=i: e.tensor_tensor(out=ltmp[:, 2 * i, :], in0=ltmp[:, 2 * i, :], in1=ltmp[:, 2 * i + 1, :],
                                                          op=ALU.subtract), reads=[lbuf], writes=[lbuf])
                S.op("act", lambda e, i=i: e.activation(out=self.lb[:, 2 * i, :], in_=ltmp[:, 2 * i, :], func=AF.Sigmoid),
                     reads=[lbuf], writes=[self.cbuf])
                S.op("dve", lambda e, i=i: e.tensor_scalar(self.lb[:, 2 * i + 1, :], self.lb[:, 2 * i, :], -1.0, 1.0,
                                                          op0=ALU.mult, op1=ALU.add), reads=[self.cbuf], writes=[self.cbuf])
            self.load_fm(st, "gain", io["a_norm_g"], 128, self.gain[:, 0:1], self.cbuf)

            craw = self.sb(st, "craw", [128, 2, DC], F32)
            cT = self.sb(st, "cT", [128, DC, 2], BF16)
            cb = Buf()
            for r in range(2):
                self.load_fm(st, "c%d" % r, io["cvec"][r], D, craw[:, r, :], cb)
            for r in range(2):
                S.op("act", lambda e, r=r: e.activation(out=cT[:, :, r], in_=craw[:, r, :], func=AF.Silu), reads=[cb], writes=[cb])
            badaT = self.sb(st, "badaT", [128, 6 * DC], F32)
            bb = Buf()
            self.load_fm(st, "bada", io["b_ada"], 6 * D, badaT[:], bb)

            def epi(idx, tag, banks, NB):
                pt, pb = banks[0]
                S.op("dve", lambda e: e.tensor_tensor(out=self.modx[:, idx:idx + 1], in0=pt[:, 0:1], in1=badaT[:, idx:idx + 1],
                                                      op=ALU.add), reads=[pb, bb], writes=[self.mbuf])
                S.op("dve", lambda e: e.tensor_tensor(out=self.modc[:, idx:idx + 1], in0=pt[:, 1:2], in1=badaT[:, idx:idx + 1],
                                                      op=ALU.add), reads=[pb, bb], writes=[self.mbuf])
            chunks = [(io["w_ada"][:, oc * 128:(oc + 1) * 128], None) for oc in range(6 * DC)]
            ga = self.gemm_f_gen(st, "ada", cT, cb, DC, 2, chunks, epi)
            gr = self.phase_rows0_gen()
            na, nr = 6 * DC, cfg.nc_ + 2 * cfg.nt
            ia = 0
            for j in range(nr):
                tgt = ((j + 1) * na) // nr
                while ia < tgt:
                    next(ga, None)
                    ia += 1
                next(gr, None)
            for _ in ga:
                pass
            for _ in gr:
                pass
            for mt in (self.modx, self.modc):
                for m in (1, 4):
                    S.op("dve", lambda e, mt=mt, m=m: e.tensor_scalar(mt[:, m * DC:(m + 1) * DC], mt[:, m * DC:(m + 1) * DC], 1.0, None,
                                                                     op0=ALU.add), reads=[self.mbuf], writes=[self.mbuf])

    def ln_stats(self, xt, xb, stt, sb_, D):
        S = self.S
        nchunk = (D + 511) // 512
        for c in range(nchunk):
            S.op("dve", lambda e, c=c: e.bn_stats(stt["st"][:, c, :], xt[:, c * 512:min(D, (c + 1) * 512)]), reads=[xb], writes=[sb_])
        S.op("dve", lambda e: e.bn_aggr(stt["mv"][:], stt["st"][:, 0:nchunk, :]), reads=[sb_], writes=[sb_])
        S.op("act", lambda e: e.activation(out=stt["sd"][:], in_=stt["mv"][:, 1:2], func=AF.Sqrt, bias=stt["eps"][:, 0:1], scale=1.0),
             reads=[sb_, self.cbuf], writes=[sb_])
        S.op("dve", lambda e: e.reciprocal(stt["rstd"][:], stt["sd"][:]), reads=[sb_], writes=[sb_])
        S.op("dve", lambda e: e.scalar_tensor_tensor(out=stt["nmr"][:], in0=stt["mv"][:, 0:1], scalar=-1.0, in1=stt["rstd"][:],
                                                     op0=ALU.mult, op1=ALU.mult), reads=[sb_], writes=[sb_])

    def mk_stats(self, st, name):
        stt = {"st": self.sb(st, name + "_st", [128, 8, 6], F32), "mv": self.sb(st, name + "_mv", [128, 2], F32),
               "sd": self.sb(st, name + "_sd", [128, 1], F32), "rstd": self.sb(st, name + "_rs", [128, 1], F32),
               "nmr": self.sb(st, name + "_nm", [128, 1], F32), "eps": self.sb(st, name + "_ep", [128, 1], F32)}
        self.S.op("dve", lambda e: e.memset(stt["eps"][:], LN_EPS), writes=[self.cbuf])
        return stt

    def add_pos(self, xt, xb, n, res):
        cfg, S, io = self.cfg, self.S, self.io
        D = cfg.D
        HB = D // 2
        BWD = min(512, HB)
        sel = res["sel"][n % 2]
        sbf = res["selb"][n % 2]
        S.dma("sp", sel[:], io["rowsel"][n], writes=[sbf])
        k = 0
        for part in range(2):
            lhs = sel if part == 0 else res["csel"]
            for c0 in range(0, HB, BWD):
                pp, ppb = res["pp"][k % 2], res["ppb"][k % 2]
                k += 1
                S.op("pe", lambda e, lhs=lhs, c0=c0, pp=pp: e.matmul(pp[:, 0:BWD], lhsT=lhs[:], rhs=self.Etab[:, c0:c0 + BWD],
                                                                    start=True, stop=True),
                     reads=[sbf, res["cselb"], self.ebuf], writes=[ppb])
                o0 = part * HB + c0
                S.op("dve", lambda e, o0=o0, pp=pp: e.tensor_tensor(out=xt[:, o0:o0 + BWD], in0=xt[:, o0:o0 + BWD], in1=pp[:, 0:BWD],
                                                                   op=ALU.add), reads=[ppb, xb], writes=[xb])

    def mk_pos(self, st):
        res = {"sel": [self.sb(st, "sel%d" % i, [64, 128], F32) for i in range(2)], "selb": [Buf(), Buf()],
               "csel": self.sb(st, "csel", [64, 128], F32), "cselb": Buf(),
               "pp": [self.ps(st, "pp%d" % i, [128, 512], F32) for i in range(2)], "ppb": [Buf(), Buf()]}
        self.S.dma("sp", res["csel"][:], self.io["colsel"], writes=[res["cselb"]])
        return res

    def norm_T_store(self, xt, xb, stt, sb_, hn, hnb, hT, hTb, tp, tpb, mods, m_shift, m_scale, dst, t0):
        cfg, S = self.cfg, self.S
        D, DC = cfg.D, cfg.DC
        S.op("act", lambda e: e.activation(out=hn[:], in_=xt[:], func=AF.Identity, bias=stt["nmr"][:, 0:1], scale=stt["rstd"][:, 0:1]),
             reads=[xb, sb_], writes=[hnb])
        for dc in range(DC):
            p, pb = tp[dc % 2], tpb[dc % 2]
            S.op("pe", lambda e, dc=dc, p=p: e.transpose(p[:], hn[:, dc * 128:(dc + 1) * 128], self.ident_b[:]),
                 reads=[hnb, self.cbuf], writes=[pb])
            if mods is None:
                S.op("dve", lambda e, dc=dc, p=p: e.tensor_copy(hT[:, dc, :], p[:]), reads=[pb], writes=[hTb])
            else:
                S.op("dve", lambda e, dc=dc, p=p: e.tensor_scalar(hT[:, dc, :], p[:], mods[:, m_scale * DC + dc:m_scale * DC + dc + 1],
                                                                 mods[:, m_shift * DC + dc:m_shift * DC + dc + 1], op0=ALU.mult, op1=ALU.add),
                     reads=[pb, self.mbuf], writes=[hTb])
        S.dma("sp", dst[:, t0:t0 + 128].rearrange("(c p) t -> p c t", p=128), hT[:], reads=[hTb])

    def phase_rows0_gen(self):
        cfg, S, io, d = self.cfg, self.S, self.io, self.d
        D, DC, NT, CTX = cfg.D, cfg.DC, cfg.NT, cfg.CTX
        with ExitStack() as st:
            xts = [self.sb(st, "r0x%d" % i, [128, D], F32) for i in range(2)]
            hns = [self.sb(st, "r0h%d" % i, [128, D], BF16) for i in range(2)]
            hTs = [self.sb(st, "r0t%d" % i, [128, DC, 128], BF16) for i in range(2)]
            xbs, hbs, tbs = [Buf(), Buf()], [Buf(), Buf()], [Buf(), Buf()]
            tp = [self.ps(st, "r0p%d" % i, [128, 128], BF16) for i in range(2)]
            tpb = [Buf(), Buf()]
            stt = self.mk_stats(st, "r0")
            sb_ = Buf()
            pos = self.mk_pos(st)
            jobs = [("c", n) for n in range(cfg.nc_)] + [("x", n) for n in range(2 * cfg.nt)]
            for j, (kind, n) in enumerate(jobs):
                xt, xb, hn, hnb, hT, hTb = xts[j % 2], xbs[j % 2], hns[j % 2], hbs[j % 2], hTs[j % 2], tbs[j % 2]
                src = io["ctxl"] if kind == "c" else io["xloc"]
                S.dma("sp", xt[:], src[n * 128:(n + 1) * 128, :], writes=[xb])
                if kind == "x":
                    self.add_pos(xt, xb, n, pos)
                self.ln_stats(xt, xb, stt, sb_, D)
                self.norm_T_store(xt, xb, stt, sb_, hn, hnb, hT, hTb, tp, tpb, None, 0, 1,
                                  d["hTc"] if kind == "c" else d["hTx"], n * 128)
                yield j

    def phase_inproj(self):
        cfg, S, io, d = self.cfg, self.S, self.io, self.d
        D, DC, NT, CTX, H, G, AW, BW = cfg.D, cfg.DC, cfg.NT, cfg.CTX, cfg.H, cfg.G, cfg.AW, cfg.BW
        NX = 2 * NT

        def fam(name, n):
            if name == "zP":
                return [(io["w_zP"][:, c * 128:(c + 1) * 128], (name, c)) for c in range(n)]
            if name == "zR":
                return [(io["w_zR"][:, c * 128:(c + 1) * 128], (name, c)) for c in range(n)]
            o = cfg.off[name]
            return [(io["w_in"][:, o + c * 128:o + (c + 1) * 128], (name, c)) for c in range(n)]

        def run(setname, xsrc, T, fams, col0):
            with ExitStack() as st:
                xt, xb = self.load_xt(st, "ipx", xsrc, DC, T)
                mods = self.modc if setname == "ctx" else self.modx
                for dc in range(DC):
                    S.op("dve", lambda e, dc=dc: e.tensor_scalar(xt[:, dc, :], xt[:, dc, :], mods[:, DC + dc:DC + dc + 1], mods[:, dc:dc + 1],
                                                                op0=ALU.mult, op1=ALU.add), reads=[xb, self.mbuf], writes=[xb])
                NBmax = min(T, 512)
                sf = [self.sb(st, "ipsf%d" % i, [128, T], F32) for i in range(2)]
                sh = [self.sb(st, "ipsh%d" % i, [128, T], BF16) for i in range(2)]
                sg = [self.sb(st, "ipsg%d" % i, [128, T], F32) for i in range(2)]
                sfb, shb, sgb = [Buf(), Buf()], [Buf(), Buf()], [Buf(), Buf()]

                def epi(idx, tag, banks, NB):
                    name, c = tag
                    i2 = idx % 2
                    rows = slice(c * 128, (c + 1) * 128)
                    for b, (pt, pb) in enumerate(banks):
                        cs = slice(b * NB, (b + 1) * NB)
                        if name in ("q", "g", "ga", "gb"):
                            fn = AF.Silu if name in ("q", "g") else AF.Sigmoid
                            S.op("act", lambda e, pt=pt, cs=cs, fn=fn: e.activation(out=sh[i2][:, cs], in_=pt[:, 0:NB], func=fn),
                                 reads=[pb], writes=[shb[i2]])
                        elif name in ("i", "u"):
                            S.op("act", lambda e, pt=pt, cs=cs: e.activation(out=sh[i2][:, cs], in_=pt[:, 0:NB], func=AF.Copy),
                                 reads=[pb], writes=[shb[i2]])
                        elif name == "v":
                            S.op("act", lambda e, pt=pt, cs=cs: e.activation(out=sf[i2][:, cs], in_=pt[:, 0:NB], func=AF.Copy),
                                 reads=[pb], writes=[sfb[i2]])
                        else:
                            li = 0 if name == "zP" else 2
                            S.op("act", lambda e, pt=pt, cs=cs: e.activation(out=sg[i2][:, cs], in_=pt[:, 0:NB], func=AF.Sigmoid),
                                 reads=[pb], writes=[sgb[i2]])
                            S.op("dve", lambda e, cs=cs, li=li: e.tensor_scalar(sg[i2][:, cs], sg[i2][:, cs], self.lb[:, li + 1, c:c + 1],
                                                                               self.lb[:, li, c:c + 1], op0=ALU.mult, op1=ALU.add),
                                 reads=[sgb[i2], self.cbuf], writes=[sgb[i2]])
                            S.op("act", lambda e, cs=cs: e.activation(out=sf[i2][:, cs], in_=sg[i2][:, cs], func=AF.Ln),
                                 reads=[sgb[i2]], writes=[sfb[i2]])
                            S.op("dve", lambda e, cs=cs: e.tensor_scalar(sh[i2][:, cs], sg[i2][:, cs], -1.0, 1.0, op0=ALU.mult, op1=ALU.add),
                                 reads=[sgb[i2]], writes=[shb[i2]])
                    if name == "q":
                        S.dma("sp", d["qsT"][rows, :], sh[i2][:], reads=[shb[i2]])
                    elif name == "g":
                        S.dma("sp", d["gsT"][rows, :], sh[i2][:], reads=[shb[i2]])
                    elif name == "ga":
                        S.dma("sp", d["sgaT"][rows, :], sh[i2][:], reads=[shb[i2]])
                    elif name == "gb":
                        S.dma("sp", d["sgbT"][rows, :], sh[i2][:], reads=[shb[i2]])
                    elif name == "u":
                        S.dma("sp", d["uT"][rows, :], sh[i2][:], reads=[shb[i2]])
                    elif name == "i":
                        S.dma("sp", d["iT"][rows, col0["i"]:col0["i"] + T], sh[i2][:], reads=[shb[i2]])
                    elif name == "v":
                        S.dma("sp", d["vT"][rows, :], sf[i2][:], reads=[sfb[i2]])
                    else:
                        lf, kk = ("lfP", "kP") if name == "zP" else ("lfR", "kR")
                        S.dma("sp", d[lf][rows, col0[name]:col0[name] + T], sf[i2][:], reads=[sfb[i2]])
                        S.dma("sp", d[kk][rows, col0[name]:col0[name] + T], sh[i2][:], reads=[shb[i2]])

                chunks = []
                for f in fams:
                    n = {"q": H, "i": H, "zP": H, "zR": H, "g": H, "u": G, "v": G, "ga": DC, "gb": DC}[f]
                    chunks += fam(f, n)
                self.gemm_f(st, "ip", xt, xb, DC, T, chunks, epi)
            S.barrier()

        run("ctx", d["hTc"], CTX, ["i", "zP", "zR"], {"i": 0, "zP": 0, "zR": 0})
        run("oth", d["hTx"][:, NT:NX], NT, ["i", "zR"], {"i": CTX + NT, "zR": CTX + NT})
        run("own", d["hTx"][:, 0:NT], NT, ["q", "i", "zP", "zR", "g", "u", "v", "ga", "gb"],
            {"i": CTX, "zP": CTX, "zR": CTX})

    def phase_scan(self):
        cfg, S, io, d = self.cfg, self.S, self.io, self.d
        NT, CTX, H = cfg.NT, cfg.CTX, cfg.H
        nt, nct = cfg.nt, cfg.nc_
        NX = 2 * NT
        NA = CTX + NX
        NP_ = CTX + NT
        QS = 128.0 ** -0.5
        for oc in range(cfg.DC):
            for g in range(cfg.DFF // 4096 if cfg.DFF >= 4096 else 1):
                kgs = min(4096, cfg.DFF)
                S.dma("pool", d["w2t"][oc * 128:(oc + 1) * 128, g * kgs:(g + 1) * kgs].rearrange("p (c f) -> p c f", f=128),
                      io["w_ff2"][g * kgs:(g + 1) * kgs, oc * 128:(oc + 1) * 128].rearrange("(c p) f -> p c f", p=128), writes=[self.w2tb])

        def vw(t, col0, dims):
            return bass.AP(t, col0, [[t.shape[1], 128]] + [list(x) for x in dims])

        with ExitStack() as st:
            qs = self.sb(st, "sc_qs", [128, NT], BF16)
            gs = self.sb(st, "sc_gs", [128, NT], BF16)
            iT = self.sb(st, "sc_iT", [128, NA], BF16)
            Lf = {"P": self.sb(st, "sc_lfP", [128, NP_], F32), "R": self.sb(st, "sc_lfR", [128, NA], F32)}
            Kk = {"P": self.sb(st, "sc_kP", [128, NP_], BF16), "R": self.sb(st, "sc_kR", [128, NA], BF16)}
            bq, bg, bi = Buf(), Buf(), Buf()
            bL = {"P": Buf(), "R": Buf()}
            bK = {"P": Buf(), "R": Buf()}
            ya = self.sb(st, "sc_ya", [128, NT], BF16)
            yab = Buf()
            rmask = self.sb(st, "sc_rmask", [128, NA], BF16)
            S.op("dve", lambda e: e.memset(rmask[:], 1.0), writes=[self.cbuf])
            S.op("dve", lambda e: e.memset(vw(rmask, 0, [[128, NA // 128], [1, 1]]), 0.0), writes=[self.cbuf])
            tB = self.sb(st, "sc_tB", [128, NA], F32)
            tE = self.sb(st, "sc_tE", [128, NA], F32)
            tD = self.sb(st, "sc_tD", [128, NT], F32)
            tE2 = self.sb(st, "sc_tE2", [128, NT], F32)
            bB, bE, bD, bE2 = Buf(), Buf(), Buf(), Buf()
            NTL = {"P": NP_ // 128, "R": NA // 128}
            b127 = {dd: self.sb(st, "sc_b127" + dd, [128, NTL[dd]], F32) for dd in "PR"}
            dl = {dd: self.sb(st, "sc_dl" + dd, [128, NTL[dd]], F32) for dd in "PR"}
            bdl = {"P": Buf(), "R": Buf()}
            Ks = {"P": self.sb(st, "sc_KsP", [128, NP_], BF16), "R": self.sb(st, "sc_KsR", [128, NA], BF16)}
            bKs = {"P": Buf(), "R": Buf()}
            Qi = {dd: self.sb(st, "sc_Qi" + dd, [128, NT], BF16) for dd in "PR"}
            Qn = {dd: self.sb(st, "sc_Qn" + dd, [128, NT], BF16) for dd in "PR"}
            Kn = {dd: self.sb(st, "sc_Kn" + dd, [128, NT], BF16) for dd in "PR"}
            Qc = {dd: self.sb(st, "sc_Qc" + dd, [128, NT // 2], BF16) for dd in "PR"}
            Kc = {dd: self.sb(st, "sc_Kc" + dd, [128, NT // 2], BF16) for dd in "PR"}
            bQ = {"P": Buf(), "R": Buf()}
            SscR = self.sb(st, "sc_SscR", [128, NT], BF16)
            bSR = Buf()
            Sst = {"P": self.sb(st, "sc_SP", [128, 128], F32), "R": self.sb(st, "sc_SR", [128, 128], F32)}
            Sb = {"P": Buf(), "R": Buf()}
            NW = 2
            mk = lambda nm, shp, dt: [self.sb(st, "sc_%s%d" % (nm, i), shp, dt) for i in range(NW)]
            KV = mk("KV", [64, 4, 128], BF16)
            Ssc = mk("Ssc", [128, 128], BF16)
            attP = mk("attP", [64, 2, 128], BF16)
            attR = mk("attR", [64, 2, 128], BF16)
            on = mk("on", [128, 128], BF16)
            sq = mk("sq", [128, 128], F32)
            sv = mk("sv", [128, 6], F32)
            wb = [Buf() for _ in range(NW)]
            for i in range(NW):
                S.op("dve", lambda e, i=i: e.memset(attP[i][:], 0.0), writes=[wb[i]])
                S.op("dve", lambda e, i=i: e.memset(attR[i][:], 0.0), writes=[wb[i]])
            ptb = [self.ps(st, "sc_pt%d" % i, [64, 4, 128], BF16) for i in range(2)]
            ptbb = [Buf(), Buf()]
            pU = self.ps(st, "sc_pU", [128, 128], F32)
            pUb = Buf()
            pA = [self.ps(st, "sc_pA%d" % i, [64, 4, 128], F32) for i in range(2)]
            pAb = [Buf(), Buf()]
            pO = [self.ps(st, "sc_pO%d" % i, [128, 128], F32) for i in range(2)]
            pOb = [Buf(), Buf()]
            pY = self.ps(st, "sc_pY", [128, 128], BF16)
            pYb = Buf()
            cnt = [0]
            mP, mR = self.maskP[0:64, 0:64], self.maskR[0:64, 0:64]

            def prep_ops(dd):
                N, ntl = (NP_, NTL["P"]) if dd == "P" else (NA, NTL["R"])
                L, kk, o0 = Lf[dd], Kk[dd], CTX
                sgn, ia, ic = (1.0, 31, 63) if dd == "P" else (-1.0, 32, 64)
                full3 = lambda t, c0=0, n=ntl: vw(t, c0, [[128, n], [1, 128]])
                own_h = lambda t, c0: vw(t, c0, [[64, 2 * nt], [1, 64]])
                half = lambda t, c0, hoff: vw(t, c0 + hoff, [[128, nt], [1, 64]])
                cmp_ = lambda t: vw(t, 0, [[64, nt], [1, 64]])
                ops = []
                A = ops.append
                A(lambda: S.op("dve", lambda e: e.tensor_tensor_scan(out=tB[:, 0:N], data0=rmask[:, 0:N], data1=L[:, 0:N], initial=0.0,
                                                                   op0=ALU.mult, op1=ALU.add), reads=[bL[dd], self.cbuf], writes=[bB]))
                A(lambda: S.op("dve", lambda e: e.tensor_copy(b127[dd][:], vw(tB, 127, [[128, ntl]])), reads=[bB], writes=[bdl[dd]]))
                A(lambda: S.op("act", lambda e: e.activation(out=dl[dd][:], in_=b127[dd][:], func=AF.Exp), reads=[bdl[dd]], writes=[bdl[dd]]))
                if dd == "R":
                    A(lambda: S.op("dve", lambda e: e.tensor_tensor(out=tB[:, 0:N], in0=tB[:, 0:N], in1=L[:, 0:N], op=ALU.subtract),
                                   reads=[bB, bL[dd]], writes=[bB]))
                    A(lambda: S.op("act", lambda e: e.activation(out=tE[:, 0:N], in_=tB[:, 0:N], func=AF.Exp), reads=[bB], writes=[bE]))
                else:
                    A(lambda: S.op("dve", lambda e: e.tensor_tensor(out=full3(tE), in0=vw(b127[dd], 0, [[1, ntl], [0, 128]]), in1=full3(tB),
                                                                    op=ALU.subtract), reads=[bB, bdl[dd]], writes=[bE]))
                    A(lambda: S.op("act", lambda e: e.activation(out=tE[:, 0:N], in_=tE[:, 0:N], func=AF.Exp), reads=[bE], writes=[bE]))
                A(lambda: S.op("dve", lambda e: e.tensor_tensor(out=Ks[dd][:, 0:N], in0=kk[:, 0:N], in1=tE[:, 0:N], op=ALU.mult),
                               reads=[bE, bK[dd]], writes=[bKs[dd]]))
                if dd == "P":
                    A(lambda: S.op("act", lambda e: e.activation(out=tE2[:], in_=tB[:, o0:o0 + NT], func=AF.Exp), reads=[bB], writes=[bE2]))
                else:
                    A(lambda: S.op("dve", lambda e: e.tensor_tensor(out=full3(tD, 0, nt), in0=vw(b127[dd], nct, [[1, nt], [0, 128]]),
                                                                    in1=full3(tB, o0, nt), op=ALU.subtract), reads=[bB, bdl[dd]], writes=[bD]))
                    A(lambda: S.op("act", lambda e: e.activation(out=tE2[:], in_=tD[:], func=AF.Exp), reads=[bD], writes=[bE2]))
                A(lambda: S.op("dve", lambda e: e.tensor_tensor(out=Qi[dd][:], in0=qs[:], in1=tE2[:], op=ALU.mult), reads=[bE2, bq], writes=[bQ[dd]]))
                A(lambda: S.op("dve", lambda e: e.tensor_tensor(out=own_h(tD, 0), in0=own_h(tB, o0), in1=vw(tB, o0 + ia, [[64, 2 * nt], [0, 64]]),
                                                                op=ALU.subtract), reads=[bB, bE2], writes=[bD]))
                A(lambda: S.op("act", lambda e: e.activation(out=tE2[:], in_=tD[:], func=AF.Exp, scale=sgn), reads=[bD, bQ[dd]], writes=[bE2]))
                A(lambda: S.op("dve", lambda e: e.tensor_tensor(out=Qn[dd][:], in0=qs[:], in1=tE2[:], op=ALU.mult), reads=[bE2, bq], writes=[bQ[dd]]))
                A(lambda: S.op("act", lambda e: e.activation(out=tE2[:], in_=tD[:], func=AF.Exp, scale=-sgn), reads=[bD, bQ[dd]], writes=[bE2]))
                A(lambda: S.op("dve", lambda e: e.tensor_tensor(out=Kn[dd][:], in0=kk[:, o0:o0 + NT], in1=tE2[:], op=ALU.mult),
                               reads=[bE2, bK[dd]], writes=[bQ[dd]]))
                A(lambda: S.op("dve", lambda e: e.tensor_tensor(out=full3(tD, 0, nt), in0=full3(tB, o0, nt), in1=vw(tB, o0 + ic, [[128, nt], [0, 128]]),
                                                                op=ALU.subtract), reads=[bB, bE2], writes=[bD]))
                qh, kh = (64, 0) if dd == "P" else (0, 64)
                A(lambda: S.op("act", lambda e: e.activation(out=cmp_(tE2), in_=half(tD, 0, qh), func=AF.Exp, scale=sgn), reads=[bD, bQ[dd]], writes=[bE2]))
                A(lambda: S.op("dve", lambda e: e.tensor_tensor(out=cmp_(Qc[dd]), in0=half(qs, 0, qh), in1=cmp_(tE2), op=ALU.mult),
                               reads=[bE2, bq], writes=[bQ[dd]]))
                A(lambda: S.op("act", lambda e: e.activation(out=cmp_(tE2), in_=half(tD, 0, kh), func=AF.Exp, scale=-sgn), reads=[bD, bQ[dd]], writes=[bE2]))
                A(lambda: S.op("dve", lambda e: e.tensor_tensor(out=cmp_(Kc[dd]), in0=half(kk, o0, kh), in1=cmp_(tE2), op=ALU.mult),
                               reads=[bE2, bK[dd]], writes=[bQ[dd]]))
                return ops

            def state_update(dd, col, til, slot):
                w = wb[slot]
                k2 = cnt[0] % 2
                cnt[0] += 1
                for hh in range(2):
                    S.op("pe", lambda e, hh=hh: e.transpose(ptb[k2][:, hh, :], Ks[dd][:, col + hh * 64:col + (hh + 1) * 64], self.ident_b[:]),
                         reads=[bKs[dd], self.cbuf], writes=[ptbb[k2]])
                    S.op("pe", lambda e, hh=hh: e.transpose(ptb[k2][:, 2 + hh, :], iT[:, col + hh * 64:col + (hh + 1) * 64], self.ident_b[:]),
                         reads=[bi, self.cbuf], writes=[ptbb[k2]])
                S.op("act", lambda e: e.activation(out=KV[slot][:], in_=ptb[k2][:], func=AF.Copy), reads=[ptbb[k2]], writes=[w])
                for hh in range(2):
                    S.op("pe", lambda e, hh=hh: e.matmul(pU[:], lhsT=KV[slot][:, hh, :], rhs=KV[slot][:, 2 + hh, :], start=(hh == 0), stop=(hh == 1)),
                         reads=[w], writes=[pUb])
                S.op("dve", lambda e: e.scalar_tensor_tensor(out=Sst[dd][:], in0=Sst[dd][:], scalar=dl[dd][:, til:til + 1], in1=pU[:],
                                                             op0=ALU.mult, op1=ALU.add), reads=[pUb, bdl[dd], Sb[dd]], writes=[Sb[dd]])

            for h in range(H):
                rows = slice(h * 128, (h + 1) * 128)
                for (dst_, src_, bf) in ((Lf["R"], d["lfR"], bL["R"]), (Kk["R"], d["kR"], bK["R"]), (iT, d["iT"], bi), (qs, d["qsT"], bq),
                                         (Lf["P"], d["lfP"], bL["P"]), (Kk["P"], d["kP"], bK["P"]), (gs, d["gsT"], bg)):
                    S.dma("sp", dst_[:], src_[rows, :], writes=[bf])
                for dd in ("P", "R"):
                    S.op("dve", lambda e, dd=dd: e.memset(Sst[dd][:], 0.0), writes=[Sb[dd]])
                for f in prep_ops("R"):
                    f()
                pops = prep_ops("P")
                seq = [("c", n) for n in reversed(range(nct))] + [("x", n) for n in reversed(range(nt, 2 * nt))] + \
                      [("x", n) for n in reversed(range(nt))]
                for j, (kind, n) in enumerate(seq):
                    slot = j % NW
                    til = n if kind == "c" else nct + n
                    col = til * 128
                    if kind == "x" and n < nt:
                        oc = slice(n * 128, (n + 1) * 128)
                        S.op("act", lambda e, oc=oc: e.activation(out=SscR[:, oc], in_=Sst["R"][:], func=AF.Copy), reads=[Sb["R"]], writes=[bSR])
                    state_update("R", col, til, slot)
                    if pops:
                        pops.pop(0)()
                while pops:
                    pops.pop(0)()
                for n in range(nct):
                    state_update("P", n * 128, n, n % NW)
                for n in range(nt):
                    slot = n % NW
                    w = wb[slot]
                    oc = slice(n * 128, (n + 1) * 128)
                    och = slice(n * 64, (n + 1) * 64)
                    ocA, ocB = slice(n * 128, n * 128 + 64), slice(n * 128 + 64, (n + 1) * 128)
                    til = nct + n
                    col = til * 128
                    k2 = n % 2
                    S.op("act", lambda e, slot=slot: e.activation(out=Ssc[slot][:], in_=Sst["P"][:], func=AF.Copy), reads=[Sb["P"]], writes=[w])
                    pa = pA[k2]
                    blocks = [("P", 0, slice(0, 64), Kn, Qn, ocA), ("P", 0, slice(64, 128), Kc, Qc, och), ("P", 1, slice(64, 128), Kn, Qn, ocB),
                              ("R", 2, slice(0, 64), Kn, Qn, ocA), ("R", 3, slice(64, 128), Kn, Qn, ocB), ("R", 3, slice(0, 64), Kc, Qc, och)]
                    for (dd, row, cs, KK, QQ, sl) in blocks:
                        S.op("pe", lambda e, dd=dd, row=row, cs=cs, KK=KK, QQ=QQ, sl=sl, pa=pa: e.matmul(
                            pa[:, row, cs], lhsT=KK[dd][:, sl], rhs=QQ[dd][:, sl], start=True, stop=True), reads=[bQ[dd]], writes=[pAb[k2]])
                    aP, aR = attP[slot], attR[slot]
                    S.op("dve", lambda e, aP=aP, pa=pa: e.copy_predicated(aP[:, 0, 0:64], mP, pa[:, 0, 0:64]), reads=[pAb[k2], self.cbuf], writes=[w])
                    S.op("act", lambda e, aP=aP, pa=pa: e.activation(out=aP[:, 0, 64:128], in_=pa[:, 0, 64:128], func=AF.Copy), reads=[pAb[k2]], writes=[w])
                    S.op("dve", lambda e, aP=aP, pa=pa: e.copy_predicated(aP[:, 1, 64:128], mP, pa[:, 1, 64:128]), reads=[pAb[k2], self.cbuf], writes=[w])
                    S.op("dve", lambda e, aR=aR, pa=pa: e.copy_predicated(aR[:, 0, 0:64], mR, pa[:, 2, 0:64]), reads=[pAb[k2], self.cbuf], writes=[w])
                    S.op("act", lambda e, aR=aR, pa=pa: e.activation(out=aR[:, 1, 0:64], in_=pa[:, 3, 0:64], func=AF.Copy), reads=[pAb[k2]], writes=[w])
                    S.op("dve", lambda e, aR=aR, pa=pa: e.copy_predicated(aR[:, 1, 64:128], mR, pa[:, 3, 64:128]), reads=[pAb[k2], self.cbuf], writes=[w])
                    state_update("P", col, til, slot)
                    po = pO[k2]
                    kv = KV[slot]
                    mm = [(aP[:, 0, :], kv[:, 2, :], [w]), (aP[:, 1, :], kv[:, 3, :], [w]), (Qi["P"][:, oc], Ssc[slot][:], [w, bQ["P"]]),
                          (aR[:, 0, :], kv[:, 2, :], [w]), (aR[:, 1, :], kv[:, 3, :], [w]), (Qi["R"][:, oc], SscR[:, oc], [bQ["R"], bSR])]
                    for j, (l_, r_, rd) in enumerate(mm):
                        S.op("pe", lambda e, l_=l_, r_=r_, j=j, po=po: e.matmul(po[:], lhsT=l_, rhs=r_, start=(j == 0), stop=(j == len(mm) - 1)),
                             reads=rd, writes=[pOb[k2]])
                    s4 = sv[slot]
                    S.op("act", lambda e, slot=slot, po=po, s4=s4: e.activation(out=sq[slot][:], in_=po[:], func=AF.Square, accum_out=s4[:, 0:1]),
                         reads=[pOb[k2], w], writes=[w])
                    S.op("dve", lambda e, s4=s4: e.tensor_scalar(s4[:, 1:2], s4[:, 0:1], QS * QS / 128.0, LN_EPS, op0=ALU.mult, op1=ALU.add), reads=[w], writes=[w])
                    S.op("act", lambda e, s4=s4: e.activation(out=s4[:, 2:3], in_=s4[:, 1:2], func=AF.Sqrt), reads=[w], writes=[w])
                    S.op("dve", lambda e, s4=s4: e.reciprocal(s4[:, 3:4], s4[:, 2:3]), reads=[w], writes=[w])
                    S.op("dve", lambda e, s4=s4: e.tensor_scalar(s4[:, 4:5], s4[:, 3:4], QS, None, op0=ALU.mult), reads=[w], writes=[w])
                    S.op("act", lambda e, slot=slot, po=po, s4=s4: e.activation(out=on[slot][:], in_=po[:], func=AF.Identity, scale=s4[:, 4:5]),
                         reads=[pOb[k2], w], writes=[w])
                    S.op("pe", lambda e, slot=slot: e.transpose(pY[:], on[slot][:], self.ident_b[:]), reads=[w, self.cbuf], writes=[pYb])
                    S.op("dve", lambda e, oc=oc: e.scalar_tensor_tensor(out=ya[:, oc], in0=pY[:], scalar=self.gain[:, 0:1], in1=gs[:, oc],
                                                                        op0=ALU.mult, op1=ALU.mult), reads=[pYb, bg, self.cbuf], writes=[yab])
                S.dma("sp", d["yaT"][rows, :], ya[:], reads=[yab])

    def phase_mixb(self):
        cfg, S, io, d = self.cfg, self.S, self.io, self.d
        NT, G, BW = cfg.NT, cfg.G, cfg.BW
        with ExitStack() as st:
            wsf = self.sb(st, "mb_wsf", [128, G, 128], F32)
            ws = self.sb(st, "mb_ws", [128, G, 128], BF16)
            bsf = self.sb(st, "mb_bsf", [1, G * 128], F32)
            bs = self.sb(st, "mb_bs", [1, G * 128], BF16)
            onesr = self.sb(st, "mb_ones", [1, 128], BF16)
            gB = self.sb(st, "mb_gB", [128, BW], F32)
            bB = self.sb(st, "mb_bB", [128, BW], F32)
            cb = Buf()
            S.dma("sp", wsf[:], io["w_sT"].rearrange("g s t -> s g t"), writes=[cb])
            S.dma("sp", bsf[:], io["b_s"].rearrange("g t -> (g t)").rearrange("(o n) -> o n", o=1), writes=[cb])
            S.dma("sp", gB[:], bcast_rows(io["v_norm_g"], 128, BW), writes=[cb])
            S.dma("sp", bB[:], bcast_rows(io["v_norm_b"], 128, BW), writes=[cb])
            S.op("dve", lambda e: e.tensor_copy(ws[:], wsf[:]), reads=[cb], writes=[cb])
            S.op("dve", lambda e: e.tensor_copy(bs[:], bsf[:]), reads=[cb], writes=[cb])
            S.op("dve", lambda e: e.memset(onesr[:], 1.0), writes=[cb])
            vT = [self.sb(st, "mb_vT%d" % i, [128, G, 128], F32) for i in range(2)]
            uT = [self.sb(st, "mb_uT%d" % i, [128, G, 128], BF16) for i in range(2)]
            vtm = [self.sb(st, "mb_vtm%d" % i, [128, BW], F32) for i in range(2)]
            vn = [self.sb(st, "mb_vn%d" % i, [128, BW], BF16) for i in range(2)]
            yb = [self.sb(st, "mb_yb%d" % i, [128, G, 128], BF16) for i in range(2)]
            lb_, vb, nb_, ybb = [Buf(), Buf()], [Buf(), Buf()], [Buf(), Buf()], [Buf(), Buf()]
            pt = [self.ps(st, "mb_pt%d" % i, [128, 128], F32) for i in range(2)]
            ptb = [Buf(), Buf()]
            pm = [self.ps(st, "mb_pm%d" % i, [128, 128], F32) for i in range(2)]
            pmb = [Buf(), Buf()]
            stt = self.mk_stats(st, "mb")
            sb_ = Buf()
            for n in range(cfg.nt):
                i2 = n % 2
                ts_ = slice(n * 128, (n + 1) * 128)
                S.dma("sp", vT[i2][:], d["vT"][:, ts_].rearrange("(g p) t -> p g t", p=128), writes=[lb_[i2]])
                S.dma("sp", uT[i2][:], d["uT"][:, ts_].rearrange("(g p) t -> p g t", p=128), writes=[lb_[i2]])
                for g in range(G):
                    k = g % 2
                    S.op("pe", lambda e, g=g, k=k: e.transpose(pt[k][:], vT[i2][:, g, :], self.ident_f[:]), reads=[lb_[i2], self.cbuf], writes=[ptb[k]])
                    S.op("act", lambda e, g=g, k=k: e.activation(out=vtm[i2][:, g * 128:(g + 1) * 128], in_=pt[k][:], func=AF.Copy),
                         reads=[ptb[k]], writes=[vb[i2]])
                self.ln_stats(vtm[i2], vb[i2], stt, sb_, BW)
                S.op("dve", lambda e: e.tensor_scalar(vtm[i2][:], vtm[i2][:], stt["rstd"][:, 0:1], stt["nmr"][:, 0:1], op0=ALU.mult, op1=ALU.add),
                     reads=[sb_, vb[i2]], writes=[vb[i2]])
                S.op("dve", lambda e: e.tensor_tensor(out=vtm[i2][:], in0=vtm[i2][:], in1=gB[:], op=ALU.mult), reads=[vb[i2], cb], writes=[vb[i2]])
                S.op("dve", lambda e: e.tensor_tensor(out=vn[i2][:], in0=vtm[i2][:], in1=bB[:], op=ALU.add), reads=[vb[i2], cb], writes=[nb_[i2]])
                for g in range(G):
                    k = g % 2
                    S.op("pe", lambda e, g=g, k=k: e.matmul(pm[k][:], lhsT=vn[i2][:, g * 128:(g + 1) * 128], rhs=ws[:, g, :], start=True, stop=False),
                         reads=[nb_[i2], cb], writes=[pmb[k]])
                    S.op("pe", lambda e, g=g, k=k: e.matmul(pm[k][:], lhsT=onesr[:], rhs=bs[:, g * 128:(g + 1) * 128], start=False, stop=True),
                         reads=[cb], writes=[pmb[k]])
                    S.op("dve", lambda e, g=g, k=k: e.tensor_tensor(out=yb[i2][:, g, :], in0=pm[k][:], in1=uT[i2][:, g, :], op=ALU.mult),
                         reads=[pmb[k], lb_[i2]], writes=[ybb[i2]])
                S.dma("sp", d["ybT"][:, ts_].rearrange("(g p) t -> p g t", p=128), yb[i2][:], reads=[ybb[i2]])

    def phase_merge(self):
        cfg, S, io, d = self.cfg, self.S, self.io, self.d
        D, DC, NT, H, G = cfg.D, cfg.DC, cfg.NT, cfg.H, cfg.G
        for which in ("a", "b"):
            with ExitStack() as st:
                KC = H if which == "a" else G
                xt, xb = self.load_xt(st, "mgx", d["yaT"] if which == "a" else d["ybT"], KC, NT)
                W = io["w_proj_a"] if which == "a" else io["w_proj_b"]
                sg = [self.sb(st, "mg_sg%d" % i, [128, NT], BF16) for i in range(2)]
                ma = [self.sb(st, "mg_ma%d" % i, [128, NT], F32) for i in range(2)]
                mo = [self.sb(st, "mg_mo%d" % i, [128, NT], BF16) for i in range(2)]
                sgb, mab, mob = [Buf(), Buf()], [Buf(), Buf()], [Buf(), Buf()]
                ma2 = [self.sb(st, "mg_m2%d" % i, [128, NT], F32) for i in range(2)]
                ma2b = [Buf(), Buf()]

                def pre(idx, tag, which=which):
                    i2 = idx % 2
                    rows = slice(idx * 128, (idx + 1) * 128)
                    S.dma("sp", sg[i2][:], d["sgaT" if which == "a" else "sgbT"][rows, :], writes=[sgb[i2]])
                    if which == "b":
                        S.dma("sp", ma[i2][:], d["mA"][rows, :], writes=[mab[i2]])

                def epi(idx, tag, banks, NB, which=which):
                    i2 = idx % 2
                    rows = slice(idx * 128, (idx + 1) * 128)
                    for b, (pt, pb) in enumerate(banks):
                        cs = slice(b * NB, (b + 1) * NB)
                        if which == "a":
                            S.op("dve", lambda e, pt=pt, cs=cs: e.tensor_tensor(out=ma[i2][:, cs], in0=pt[:, 0:NB], in1=sg[i2][:, cs], op=ALU.mult),
                                 reads=[pb, sgb[i2]], writes=[mab[i2]])
                        else:
                            S.op("dve", lambda e, pt=pt, cs=cs: e.tensor_tensor(out=ma2[i2][:, cs], in0=pt[:, 0:NB], in1=sg[i2][:, cs], op=ALU.mult),
                                 reads=[pb, sgb[i2]], writes=[ma2b[i2]])
                            S.op("dve", lambda e, cs=cs: e.tensor_tensor(out=mo[i2][:, cs], in0=ma[i2][:, cs], in1=ma2[i2][:, cs], op=ALU.add),
                                 reads=[ma2b[i2], mab[i2]], writes=[mob[i2]])
                    if which == "a":
                        S.dma("sp", d["mA"][rows, :], ma[i2][:], reads=[mab[i2]])
                    else:
                        S.dma("sp", d["mergedT"][rows, :], mo[i2][:], reads=[mob[i2]])
                chunks = [(W[:, oc * 128:(oc + 1) * 128], None) for oc in range(DC)]
                self.gemm_f(st, "mg", xt, xb, KC, NT, chunks, epi, pre=pre)
            S.barrier()

    def gemm_to_rows(self, name, xsrc, KC, W, mod_idx, dst, TBLK):
        cfg, S = self.cfg, self.S
        DC, NT = cfg.DC, cfg.NT
        for t0 in range(0, NT, TBLK):
            with ExitStack() as st:
                xt, xb = self.load_xt(st, name + "x", xsrc[:, t0:t0 + TBLK], KC, TBLK)
                NBm = min(TBLK, 512)
                nj = NBm // 128
                sf = [self.sb(st, name + "sf%d" % i, [128, NBm], F32) for i in range(2)]
                tt = [self.sb(st, name + "tt%d" % i, [128, nj, 128], F32) for i in range(2)]
                sfb, ttb = [Buf(), Buf()], [Buf(), Buf()]
                ptp = [self.ps(st, name + "pp%d" % i, [128, nj, 128], F32) for i in range(2)]
                ptb = [Buf(), Buf()]
                k = [0]

                def epi(idx, tag, banks, NB):
                    for b, (pt, pb) in enumerate(banks):
                        i2 = k[0] % 2
                        k[0] += 1
                        S.op("act", lambda e, pt=pt, i2=i2: e.activation(out=sf[i2][:, 0:NB], in_=pt[:, 0:NB], func=AF.Identity,
                                                                         scale=self.modx[:, mod_idx * DC + idx:mod_idx * DC + idx + 1]),
                             reads=[pb, self.mbuf], writes=[sfb[i2]])
                        for j in range(NB // 128):
                            S.op("pe", lambda e, j=j, i2=i2: e.transpose(ptp[i2][:, j, :], sf[i2][:, j * 128:(j + 1) * 128], self.ident_f[:]),
                                 reads=[sfb[i2], self.cbuf], writes=[ptb[i2]])
                        S.op("dve", lambda e, i2=i2: e.tensor_copy(tt[i2][:, 0:NB // 128, :], ptp[i2][:, 0:NB // 128, :]), reads=[ptb[i2]], writes=[ttb[i2]])
                        r0 = t0 + b * NB
                        S.dma("sp", dst[r0:r0 + NB, idx * 128:(idx + 1) * 128].rearrange("(j p) c -> p j c", p=128), tt[i2][:, 0:NB // 128, :],
                              reads=[ttb[i2]])
                if isinstance(W, tuple):
                    chunks = [(("tiled", W[1][oc * 128:(oc + 1) * 128, :], W[2]), None) for oc in range(DC)]
                else:
                    chunks = [(W[:, oc * 128:(oc + 1) * 128], None) for oc in range(DC)]
                self.gemm_f(st, name, xt, xb, KC, TBLK, chunks, epi, nsets=2)
            S.barrier()

    def phase_wout(self):
        cfg = self.cfg
        self.gemm_to_rows("wo", self.d["mergedT"], cfg.DC, self.io["w_out"], 2, self.d["x1pre"], min(cfg.NT, 1024))

    def phase_rows1(self):
        cfg, S, io, d = self.cfg, self.S, self.io, self.d
        D, DC, NT = cfg.D, cfg.DC, cfg.NT
        with ExitStack() as st:
            xts = [self.sb(st, "r1x%d" % i, [128, D], F32) for i in range(2)]
            pre = [self.sb(st, "r1p%d" % i, [128, D], F32) for i in range(2)]
            hn = self.sb(st, "r1h", [128, D], BF16)
            hT = self.sb(st, "r1t", [128, DC, 128], BF16)
            gB = self.sb(st, "r1g", [128, D], F32)
            bB = self.sb(st, "r1b", [128, D], F32)
            cb = Buf()
            S.dma("sp", gB[:], bcast_rows(io["ln1_g"], 128, D), writes=[cb])
            S.dma("sp", bB[:], bcast_rows(io["ln1_b"], 128, D), writes=[cb])
            xbs, pbs, hnb, hTb = [Buf(), Buf()], [Buf(), Buf()], Buf(), Buf()
            tp = [self.ps(st, "r1tp%d" % i, [128, 128], BF16) for i in range(2)]
            tpb = [Buf(), Buf()]
            stt = self.mk_stats(st, "r1")
            sb_ = Buf()
            pos = self.mk_pos(st)
            for n in range(cfg.nt):
                i2 = n % 2
                xt, xb, pr, prb = xts[i2], xbs[i2], pre[i2], pbs[i2]
                rs = slice(n * 128, (n + 1) * 128)
                S.dma("sp", xt[:], io["xloc"][rs, :], writes=[xb])
                S.dma("sp", pr[:], d["x1pre"][rs, :], writes=[prb])
                self.add_pos(xt, xb, n, pos)
                S.op("dve", lambda e, xt=xt, pr=pr: e.scalar_tensor_tensor(out=pr[:], in0=xt[:], scalar=cfg.ALPHA, in1=pr[:], op0=ALU.mult, op1=ALU.add),
                     reads=[xb, prb], writes=[prb])
                self.ln_stats(pr, prb, stt, sb_, D)
                S.op("act", lambda e, pr=pr: e.activation(out=pr[:], in_=pr[:], func=AF.Identity, bias=stt["nmr"][:, 0:1], scale=stt["rstd"][:, 0:1]),
                     reads=[sb_, prb], writes=[prb])
                S.op("dve", lambda e, pr=pr: e.tensor_tensor(out=pr[:], in0=pr[:], in1=gB[:], op=ALU.mult), reads=[prb, cb], writes=[prb])
                S.op("pool", lambda e, pr=pr: e.tensor_tensor(out=pr[:], in0=pr[:], in1=bB[:], op=ALU.add), reads=[prb, cb], writes=[prb])
                S.dma("sp", d["x1"][rs, :], pr[:], reads=[prb])
                self.ln_stats(pr, prb, stt, sb_, D)
                self.norm_T_store(pr, prb, stt, sb_, hn, hnb, hT, hTb, tp, tpb, self.modx, 3, 4, d["h2T"], n * 128)

    def phase_ff(self):
        cfg, S, io, d = self.cfg, self.S, self.io, self.d
        D, DC, NT, DFF = cfg.D, cfg.DC, cfg.NT, cfg.DFF
        with ExitStack() as st:
            xt, xb = self.load_xt(st, "f1x", d["h2T"], DC, NT)
            r_ = [self.sb(st, "f1r%d" % i, [128, NT], F32) for i in range(2)]
            hh = [self.sb(st, "f1h%d" % i, [128, NT], BF16) for i in range(2)]
            rb, hb = [Buf(), Buf()], [Buf(), Buf()]

            def epi(idx, tag, banks, NB):
                i2 = idx % 2
                for b, (pt, pb) in enumerate(banks):
                    cs = slice(b * NB, (b + 1) * NB)
                    S.op("act", lambda e, pt=pt, cs=cs: e.activation(out=r_[i2][:, cs], in_=pt[:, 0:NB], func=AF.Relu), reads=[pb], writes=[rb[i2]])
                    S.op("dve", lambda e, pt=pt, cs=cs: e.tensor_tensor(out=hh[i2][:, cs], in0=pt[:, 0:NB], in1=r_[i2][:, cs], op=ALU.mult),
                         reads=[pb, rb[i2]], writes=[hb[i2]])
                S.dma("sp", d["hidT"][idx * 128:(idx + 1) * 128, :], hh[i2][:], reads=[hb[i2]])
            chunks = [(io["w_ff1"][:, oc * 128:(oc + 1) * 128], None) for oc in range(DFF // 128)]
            self.gemm_f(st, "f1", xt, xb, DC, NT, chunks, epi)
        S.barrier()
        self.gemm_to_rows("f2", d["hidT"], DFF // 128, ("tiled", d["w2t"], self.w2tb), 5, d["x2pre"], min(NT, 512))

    def phase_rows2(self):
        cfg, S, io, d = self.cfg, self.S, self.io, self.d
        D, NT = cfg.D, cfg.NT
        with ExitStack() as st:
            x1s = [self.sb(st, "r2x%d" % i, [128, D], F32) for i in range(2)]
            pre = [self.sb(st, "r2p%d" % i, [128, D], F32) for i in range(2)]
            gB = self.sb(st, "r2g", [128, D], F32)
            bB = self.sb(st, "r2b", [128, D], F32)
            cb = Buf()
            S.dma("sp", gB[:], bcast_rows(io["ln2_g"], 128, D), writes=[cb])
            S.dma("sp", bB[:], bcast_rows(io["ln2_b"], 128, D), writes=[cb])
            xbs, pbs = [Buf(), Buf()], [Buf(), Buf()]
            stt = self.mk_stats(st, "r2")
            sb_ = Buf()
            for n in range(cfg.nt):
                i2 = n % 2
                xt, xb, pr, prb = x1s[i2], xbs[i2], pre[i2], pbs[i2]
                rs = slice(n * 128, (n + 1) * 128)
                S.dma("sp", xt[:], d["x1"][rs, :], writes=[xb])
                S.dma("sp", pr[:], d["x2pre"][rs, :], writes=[prb])
                S.op("dve", lambda e, xt=xt, pr=pr: e.scalar_tensor_tensor(out=pr[:], in0=xt[:], scalar=cfg.ALPHA, in1=pr[:], op0=ALU.mult, op1=ALU.add),
                     reads=[xb, prb], writes=[prb])
                self.ln_stats(pr, prb, stt, sb_, D)
                S.op("act", lambda e, pr=pr: e.activation(out=pr[:], in_=pr[:], func=AF.Identity, bias=stt["nmr"][:, 0:1], scale=stt["rstd"][:, 0:1]),
                     reads=[sb_, prb], writes=[prb])
                S.op("dve", lambda e, pr=pr: e.tensor_tensor(out=pr[:], in0=pr[:], in1=gB[:], op=ALU.mult), reads=[prb, cb], writes=[prb])
                S.op("pool", lambda e, pr=pr: e.tensor_tensor(out=pr[:], in0=pr[:], in1=bB[:], op=ALU.add), reads=[prb, cb], writes=[prb])
                S.dma("sp", io["out"][rs, :], pr[:], reads=[prb])


def make_in_maps(cfg, inp):
    D, SEQ, NT, CTX = cfg.D, cfg.SEQ, cfg.NT, cfg.CTX
    f = lambda a: np.ascontiguousarray(np.asarray(a, dtype=np.float32))
    x, c, ctx, c_ctx = f(inp["x"]), f(inp["c"]), f(inp["ctx"]), f(inp["c_ctx"])
    w_in = f(inp["w_in"])[0]
    lbl = f(inp["lb_logits"])
    wzf = np.ascontiguousarray(w_in[:, cfg.off["zf"]:cfg.off["zf"] + cfg.AW])
    wzb = np.ascontiguousarray(w_in[:, cfg.off["zb"]:cfg.off["zb"] + cfg.AW])
    w_s = f(inp["w_s"])[0]
    b_s = f(inp["b_s"])[0]
    w_sT = np.ascontiguousarray(w_s.transpose(0, 2, 1))
    w_sT_rev = np.ascontiguousarray(w_sT[:, ::-1, ::-1])
    b_s_rev = np.ascontiguousarray(b_s[:, ::-1])
    tri = np.arange(128)
    consts = np.stack([np.eye(128, dtype=np.float32), (tri[:, None] <= tri[None, :]).astype(np.float32),
                       (tri[:, None] >= tri[None, :]).astype(np.float32)])
    shared = {k: f(inp[k])[0] for k in ("w_ada", "b_ada", "w_proj_a", "v_norm_g", "v_norm_b", "w_proj_b", "w_out",
                                        "ln1_g", "ln1_b", "w_ff1", "w_ff2", "ln2_g", "ln2_b")}
    shared["a_norm_g"] = f(inp["a_norm_g"])[0]
    shared["w_in"] = w_in
    shared["consts"] = consts
    maps = []
    for core in range(8):
        b, half = core // 2, core % 2
        rev = half == 1
        m = dict(shared)
        gidx = np.arange(SEQ)[::-1] if rev else np.arange(SEQ)
        m["xloc"] = np.ascontiguousarray(x[b][gidx])
        m["ctxl"] = np.ascontiguousarray(ctx[b][::-1] if rev else ctx[b])
        m["cvec"] = np.stack([c[b], c_ctx])
        m["w_zP"], m["w_zR"] = (wzb, wzf) if rev else (wzf, wzb)
        m["lblP"], m["lblR"] = (lbl[1, 0:2], lbl[0, 0:2]) if rev else (lbl[0, 0:2], lbl[1, 0:2])
        m["lblP"], m["lblR"] = np.ascontiguousarray(m["lblP"]), np.ascontiguousarray(m["lblR"])
        m["w_sT"] = w_sT_rev if rev else w_sT
        m["b_s"] = b_s_rev if rev else b_s
        rows, cols = gidx // GRID_W, gidx % GRID_W
        rs = np.zeros((SEQ // 128, 64, 128), np.float32)
        tl = np.arange(SEQ)
        rs[tl // 128, rows % 64 if cfg.ROWS > 64 else rows, tl % 128] = 1.0
        m["rowsel"] = rs
        cs = np.zeros((64, 128), np.float32)
        cs[cols[:128], np.arange(128)] = 1.0
        m["colsel"] = cs
        maps.append(m)
    return maps


_CACHE = {}


def run_cfg(cfg, inp, debug_outs=(), trace=False):
    key = (cfg.D, cfg.SEQ, tuple(debug_outs))
    if key not in _CACHE:
        _CACHE[key] = Builder(cfg, debug_outs).build()
    nc = _CACHE[key]
    maps = make_in_maps(cfg, inp)
    res = run_bass_kernel_spmd(nc, maps, core_ids=list(range(8)), **({"trace": True} if trace else {}))
    out = np.zeros((cfg.BATCH, cfg.SEQ, cfg.D), np.float32)
    for core in range(8):
        b, half = core // 2, core % 2
        o = np.asarray(res.results[core]["out"], dtype=np.float32)
        if half == 0:
            out[b, :cfg.NT] = o
        else:
            out[b, cfg.NT:] = o[::-1]
    return out, res


def kernel(**inputs):
    out, _ = run_cfg(Cfg(), inputs)
    return out
```

```python
import math
from contextlib import ExitStack
import numpy as np
import concourse.bass as bass
import concourse.mybir as mybir
from concourse.bass_utils import run_bass_kernel_spmd

F32 = mybir.dt.float32
BF16 = mybir.dt.bfloat16
I32 = mybir.dt.int32
AF = mybir.ActivationFunctionType
ALU = mybir.AluOpType

LN_EPS = 1e-6
SAME_ENGINE_SYNC = True
SAME_ENGINE_WAR = False
GRID_W = 64
P = 128


class Cfg:
    def __init__(self, D=4096, SEQ=4096, BATCH=4, CTX=256, H=16, G=16, DFF=16384):
        self.D, self.SEQ, self.BATCH, self.CTX, self.H, self.G, self.DFF = D, SEQ, BATCH, CTX, H, G, DFF
        self.AW = H * 128
        self.BW = G * 128
        self.NT = SEQ // 2
        self.nt = self.NT // 128
        self.nc_ = CTX // 128
        self.DC = D // 128
        self.NMOD = 6
        self.ROWS = SEQ // GRID_W
        self.ALPHA = 2.0 ** 0.25
        sizes = (self.AW, self.AW, self.AW, self.AW, self.AW, self.BW, self.BW, D, D)
        offs = np.cumsum((0,) + sizes)
        self.off = dict(zip(("q", "i", "zf", "zb", "g", "u", "v", "ga", "gb"), offs[:-1]))
        self.NIN = int(offs[-1])


class Buf:
    __slots__ = ("w", "r")

    def __init__(self):
        self.w = None
        self.r = {}


class Sched:
    def __init__(self, nc, stack, ndma=8):
        self.nc = nc
        self.eng = {"pe": nc.tensor, "act": nc.scalar, "dve": nc.vector, "pool": nc.gpsimd, "sp": nc.sync}
        self.csem = {e: stack.enter_context(nc.semaphore("c_" + e)) for e in ("pe", "act", "dve", "pool")}
        self.cnt = {e: 0 for e in self.csem}
        self.seen = {e: {} for e in self.eng}
        self.dsem = {q: [stack.enter_context(nc.semaphore("d_%s%d" % (q, i))) for i in range(ndma)]
                     for q in ("sp", "pool")}
        self.duse = {q: [0] * ndma for q in self.dsem}
        self.dnext = {q: 0 for q in self.dsem}

    def _waits(self, eng, deps):
        need = {}
        for d in deps:
            if d is None:
                continue
            if len(d) == 4:
                if d[1] == eng and d[0] == "c" and not SAME_ENGINE_WAR:
                    continue
                d = d[:3]
            kind, key, val = d
            if kind == "c" and key == eng and (eng == "pe" or not SAME_ENGINE_SYNC):
                continue
            k = (kind, key)
            if val > need.get(k, 0):
                need[k] = val
        e = self.eng[eng]
        seen = self.seen[eng]
        for k, val in need.items():
            if seen.get(k, 0) >= val:
                continue
            seen[k] = val
            sem = self.csem[k[1]] if k[0] == "c" else k[1]
            e.wait_ge(sem, val)

    @staticmethod
    def _deps(reads, writes):
        deps = []
        for b in reads:
            deps.append(b.w)
        for b in writes:
            deps.append(b.w)
            deps.extend(t + ("war",) for t in b.r.values())
        return deps

    @staticmethod
    def _mark(tok, reads, writes):
        for b in writes:
            b.w = tok
            b.r = {}
        for b in reads:
            b.r[(tok[0], tok[1])] = tok

    def op(self, eng, fn, reads=(), writes=()):
        self._waits(eng, self._deps(reads, writes))
        ins = fn(self.eng[eng])
        self.cnt[eng] += 1
        ins.then_inc(self.csem[eng], 1)
        self._mark(("c", eng, self.cnt[eng]), reads, writes)

    def dma(self, q, out, in_, reads=(), writes=()):
        i = self.dnext[q]
        self.dnext[q] = (i + 1) % len(self.dsem[q])
        sem = self.dsem[q][i]
        deps = self._deps(reads, writes)
        if self.duse[q][i]:
            deps.append(("d", sem, 16 * self.duse[q][i]))
        self._waits(q, deps)
        self.eng[q].dma_start(out=out, in_=in_).then_inc(sem, 16)
        self.duse[q][i] += 1
        self._mark(("d", sem, 16 * self.duse[q][i]), reads, writes)

    def barrier(self):
        toks = [("c", e, self.cnt[e]) for e in self.csem if self.cnt[e]]
        for q in self.dsem:
            for sem, n in zip(self.dsem[q], self.duse[q]):
                if n:
                    toks.append(("d", sem, 16 * n))
        for e in self.eng:
            self._waits_all(e, toks)

    def _waits_all(self, eng, toks):
        e = self.eng[eng]
        seen = self.seen[eng]
        for kind, key, val in toks:
            k = (kind, key)
            if seen.get(k, 0) >= val:
                continue
            seen[k] = val
            e.wait_ge(self.csem[key] if kind == "c" else key, val)


def bcast_rows(ap1d, nrows, n):
    return bass.AP(ap1d.tensor, ap1d.offset, [[0, nrows], [1, n]])


class Builder:
    def __init__(self, cfg, debug_outs=()):
        self.cfg = cfg
        self.debug_outs = debug_outs
        self.nc = bass.Bass("TRN2", target_bir_lowering=False)
        self.stack = ExitStack()
        self.S = None

    def din(self, name, shape, dt=F32):
        return self.nc.dram_tensor(name, list(shape), dt, kind="ExternalInput").ap()

    def dscr(self, name, shape, dt):
        kind = "ExternalOutput" if name in self.debug_outs else "Internal"
        return self.nc.dram_tensor(name, list(shape), dt, kind=kind).ap()

    def _uniq(self, name):
        self._n = getattr(self, "_n", 0) + 1
        return "%s_%d" % (name, self._n)

    def sb(self, st, name, shape, dt):
        return st.enter_context(self.nc.sbuf_tensor(self._uniq(name), list(shape), dt))

    def ps(self, st, name, shape, dt=F32):
        return st.enter_context(self.nc.psum_tensor(self._uniq(name), list(shape), dt))

    def build(self):
        cfg, nc = self.cfg, self.nc
        D, NT, CTX, DC = cfg.D, cfg.NT, cfg.CTX, cfg.DC
        with self.stack as stack:
            S = self.S = Sched(nc, stack)
            io = self.io = {}
            io["xloc"] = self.din("xloc", [2 * NT, D])
            io["ctxl"] = self.din("ctxl", [CTX, D])
            io["cvec"] = self.din("cvec", [2, D])
            io["w_ada"] = self.din("w_ada", [D, 6 * D])
            io["b_ada"] = self.din("b_ada", [6 * D])
            io["w_in"] = self.din("w_in", [D, cfg.NIN])
            io["w_zP"] = self.din("w_zP", [D, cfg.AW])
            io["w_zR"] = self.din("w_zR", [D, cfg.AW])
            io["lblP"] = self.din("lblP", [2, cfg.AW])
            io["lblR"] = self.din("lblR", [2, cfg.AW])
            io["a_norm_g"] = self.din("a_norm_g", [128])
            io["w_proj_a"] = self.din("w_proj_a", [cfg.AW, D])
            io["v_norm_g"] = self.din("v_norm_g", [cfg.BW])
            io["v_norm_b"] = self.din("v_norm_b", [cfg.BW])
            io["w_sT"] = self.din("w_sT", [cfg.G, 128, 128])
            io["b_s"] = self.din("b_s", [cfg.G, 128])
            io["w_proj_b"] = self.din("w_proj_b", [cfg.BW, D])
            io["w_out"] = self.din("w_out", [D, D])
            io["ln1_g"] = self.din("ln1_g", [D])
            io["ln1_b"] = self.din("ln1_b", [D])
            io["w_ff1"] = self.din("w_ff1", [D, cfg.DFF])
            io["w_ff2"] = self.din("w_ff2", [cfg.DFF, D])
            io["ln2_g"] = self.din("ln2_g", [D])
            io["ln2_b"] = self.din("ln2_b", [D])
            io["rowsel"] = self.din("rowsel", [2 * cfg.nt, 64, 128])
            io["colsel"] = self.din("colsel", [64, 128])
            io["out"] = nc.dram_tensor("out", [NT, D], F32, kind="ExternalOutput").ap()
            self.body(stack)
            S.barrier()
        return nc


    def gemm_f(self, *a, **k):
        for _ in self.gemm_f_gen(*a, **k):
            pass

    def gemm_alloc(self, st, name, KC, T, nsets=2, nslot=3):
        KG = min(KC, 32)
        NB = min(T, 512)
        nb = T // NB
        return {"wsl": [self.sb(st, "%s_w%d" % (name, i), [128, KG, 128], BF16) for i in range(nslot)],
                "wbuf": [Buf() for _ in range(nslot)],
                "banks": [[self.ps(st, "%s_p%d_%d" % (name, s, b), [128, 512], F32) for b in range(nb)] for s in range(nsets)],
                "bbuf": [[Buf() for _ in range(nb)] for _ in range(nsets)], "wi": 0, "ci": 0}

    def gemm_f_gen(self, st, name, xt_sb, xbuf, KC, T, chunks, epi, nsets=2, pre=None, res=None, nslot=3):
        S, nc = self.S, self.nc
        KG = min(KC, 32)
        ngrp = KC // KG
        NB = min(T, 512)
        nb = T // NB
        if res is None:
            res = self.gemm_alloc(st, name, KC, T, nsets, nslot)
        wsl, wbuf, banks, bbuf = res["wsl"], res["wbuf"], res["banks"], res["bbuf"]
        nslot = len(wsl)
        xbufs = xbuf if isinstance(xbuf, list) else None
        for idx, (w_ap, tag) in enumerate(chunks):
            s = res["ci"] % nsets
            res["ci"] += 1
            if pre is not None:
                pre(idx, tag)
            for g in range(ngrp):
                sl = res["wi"] % nslot
                res["wi"] += 1
                if isinstance(w_ap, tuple):
                    S.dma("pool", wsl[sl][:], w_ap[1][:, g * KG * 128:(g + 1) * KG * 128].rearrange("p (c f) -> p c f", f=128),
                          reads=[w_ap[2]], writes=[wbuf[sl]])
                else:
                    src = w_ap[g * KG * 128:(g + 1) * KG * 128, :].rearrange("(c p) f -> p c f", p=128)
                    S.dma("pool", wsl[sl][:], src, writes=[wbuf[sl]])
                for kc in range(KG):
                    first = (g == 0 and kc == 0)
                    last = (g == ngrp - 1 and kc == KG - 1)
                    kk = g * KG + kc
                    xb_ = xbufs[min(len(xbufs) - 1, kk // max(1, KC // len(xbufs)))] if xbufs else xbuf
                    for b in range(nb):
                        S.op("pe", lambda e, s=s, b=b, sl=sl, kc=kc, kk=kk, first=first, last=last: e.matmul(
                            banks[s][b][:, 0:NB], lhsT=wsl[sl][:, kc, :],
                            rhs=xt_sb[:, kk, b * NB:(b + 1) * NB], start=first, stop=last),
                            reads=[wbuf[sl], xb_], writes=[bbuf[s][b]])
            epi(idx, tag, [(banks[s][b], bbuf[s][b]) for b in range(nb)], NB)
            yield idx

    def load_xt(self, st, name, src_ap, KC, T, xt=None, xbs=None):
        S = self.S
        if xt is None:
            xt = self.sb(st, name, [128, KC, T], BF16)
        nq = 4 if KC >= 4 else 1
        if xbs is None:
            xbs = [Buf() for _ in range(nq)]
        step = KC // nq
        for qi in range(nq):
            c0 = qi * step
            S.dma("sp", xt[:, c0:c0 + step, :],
                  src_ap[c0 * 128:(c0 + step) * 128, :].rearrange("(c p) t -> p c t", p=128), writes=[xbs[qi]])
        return xt, xbs

    def load_fm(self, st, name, vec_ap, n, dst, dbuf, c0=0):
        S = self.S
        R = n // 128
        if not hasattr(self, "_fm") or self._fm[0] is not st:
            self._fm = (st, [self.sb(st, "fm_t%d" % i, [128, 128], F32) for i in range(2)], [Buf(), Buf()],
                        [self.ps(st, "fm_p%d" % i, [128, 128], F32) for i in range(2)], [Buf(), Buf()], [0])
        _, tmps, tbs, pts, pbs, k = self._fm
        for r0 in range(0, R, 128):
            rr = min(128, R - r0)
            i = k[0] % 2
            k[0] += 1
            tmp, tb, pt, pb = tmps[i], tbs[i], pts[i], pbs[i]
            S.dma("sp", tmp[0:rr, :], vec_ap[r0 * 128:(r0 + rr) * 128].rearrange("(r c) -> r c", c=128), writes=[tb])
            S.op("pe", lambda e, rr=rr, tmp=tmp, pt=pt: e.transpose(pt[:, 0:rr], tmp[0:rr, :], self.ident_f[0:rr, 0:rr]),
                 reads=[tb, self.cbuf], writes=[pb])
            S.op("dve", lambda e, rr=rr, pt=pt, r0=r0: e.tensor_copy(dst[:, c0 + r0:c0 + r0 + rr], pt[:, 0:rr]),
                 reads=[pb], writes=[dbuf])

    def body(self, stack):
        cfg, nc, S, io = self.cfg, self.nc, self.S, self.io
        D, NT, CTX, DC, H, G = cfg.D, cfg.NT, cfg.CTX, cfg.DC, cfg.H, cfg.G
        AW, BW, DFF = cfg.AW, cfg.BW, cfg.DFF
        nt, nct = cfg.nt, cfg.nc_
        NX = 2 * NT
        d = self.d = {}
        d["hTx"] = self.dscr("hTx", [D, NX], BF16)
        d["hTc"] = self.dscr("hTc", [D, CTX], BF16)
        d["qsT"] = self.dscr("qsT", [AW, NT], BF16)
        d["gsT"] = self.dscr("gsT", [AW, NT], BF16)
        d["iT"] = self.dscr("iT", [AW, CTX + NX], BF16)
        d["lfP"] = self.dscr("lfP", [AW, CTX + NT], F32)
        d["kP"] = self.dscr("kP", [AW, CTX + NT], BF16)
        d["lfR"] = self.dscr("lfR", [AW, CTX + NX], F32)
        d["kR"] = self.dscr("kR", [AW, CTX + NX], BF16)
        d["uT"] = self.dscr("uT", [BW, NT], BF16)
        d["vT"] = self.dscr("vT", [BW, NT], F32)
        d["sgaT"] = self.dscr("sgaT", [D, NT], BF16)
        d["sgbT"] = self.dscr("sgbT", [D, NT], BF16)
        d["yaT"] = self.dscr("yaT", [AW, NT], BF16)
        d["ybT"] = self.dscr("ybT", [BW, NT], BF16)
        d["mA"] = self.dscr("mA", [D, NT], F32)
        d["mergedT"] = self.dscr("mergedT", [D, NT], BF16)
        d["x1pre"] = self.dscr("x1pre", [NT, D], F32)
        d["x1"] = self.dscr("x1", [NT, D], F32)
        d["h2T"] = self.dscr("h2T", [D, NT], BF16)
        d["hidT"] = self.dscr("hidT", [DFF, NT], BF16)
        d["x2pre"] = self.dscr("x2pre", [NT, D], F32)
        d["w2t"] = self.dscr("w2t", [D, DFF], BF16)
        self.w2tb = Buf()

        self.cbuf = Buf()
        self.ident_f = self.sb(stack, "ident_f", [128, 128], F32)
        self.ident_b = self.sb(stack, "ident_b", [128, 128], BF16)
        self.maskP = self.sb(stack, "maskP", [128, 128], I32)
        self.maskR = self.sb(stack, "maskR", [128, 128], I32)
        cin = self.din("consts", [3, 128, 128])
        ctmp = self.sb(stack, "ctmp", [128, 2, 128], F32)
        S.dma("sp", self.ident_f[:], cin[0], writes=[self.cbuf])
        S.dma("sp", ctmp[:, 0, :], cin[1], writes=[self.cbuf])
        S.dma("sp", ctmp[:, 1, :], cin[2], writes=[self.cbuf])
        S.op("dve", lambda e: e.tensor_copy(self.ident_b[:], self.ident_f[:]), reads=[self.cbuf], writes=[self.cbuf])
        S.op("dve", lambda e: e.tensor_copy(self.maskP[:], ctmp[:, 0, :]), reads=[self.cbuf], writes=[self.cbuf])
        S.op("dve", lambda e: e.tensor_copy(self.maskR[:], ctmp[:, 1, :]), reads=[self.cbuf], writes=[self.cbuf])
        self.modx = self.sb(stack, "modx", [128, 6 * DC], F32)
        self.modc = self.sb(stack, "modc", [128, 6 * DC], F32)
        self.mbuf = Buf()
        self.lb = self.sb(stack, "lb", [128, 4, H], F32)
        self.gain = self.sb(stack, "gain", [128, 1], F32)
        self.Etab = self.sb(stack, "Etab", [64, D // 2], F32)
        self.ebuf = Buf()

        self.phase_consts()
        S.barrier()
        self.phase_inproj()
        S.barrier()
        self.phase_scan()
        S.barrier()
        self.phase_mixb()
        S.barrier()
        self.phase_merge()
        S.barrier()
        self.phase_wout()
        S.barrier()
        self.phase_rows1()
        S.barrier()
        self.phase_ff()
        S.barrier()
        self.phase_rows2()

    def phase_consts(self):
        cfg, nc, S, io = self.cfg, self.nc, self.S, self.io
        D, DC, H, AW = cfg.D, cfg.DC, cfg.H, cfg.AW
        Q = D // 4
        with ExitStack() as st:
            jf = self.sb(st, "jf", [64, Q], F32)
            rf = self.sb(st, "rf", [64, 1], F32)
            ang = self.sb(st, "ang", [64, Q], F32)
            t1 = self.sb(st, "t1", [64, Q], F32)
            ki = self.sb(st, "ki", [64, Q], I32)
            b1 = Buf()
            S.op("pool", lambda e: e.iota(jf[:], [[1, Q]], base=0, channel_multiplier=0,
                                          allow_small_or_imprecise_dtypes=True), writes=[b1])
            S.op("pool", lambda e: e.iota(rf[:], [[0, 1]], base=0, channel_multiplier=1,
                                          allow_small_or_imprecise_dtypes=True), writes=[b1])
            S.op("act", lambda e: e.activation(out=ang[:], in_=jf[:], func=AF.Exp, scale=-math.log(10000.0) / Q),
                 reads=[b1], writes=[b1])
            S.op("dve", lambda e: e.tensor_scalar(ang[:], ang[:], rf[:, 0:1], None, op0=ALU.mult), reads=[b1], writes=[b1])
            S.op("dve", lambda e: e.tensor_scalar(t1[:], ang[:], 1.0 / (2 * math.pi), None, op0=ALU.mult), reads=[b1], writes=[b1])
            S.op("dve", lambda e: e.tensor_copy(ki[:], t1[:]), reads=[b1], writes=[b1])
            S.op("dve", lambda e: e.tensor_copy(t1[:], ki[:]), reads=[b1], writes=[b1])
            S.op("dve", lambda e: e.scalar_tensor_tensor(out=ang[:], in0=t1[:], scalar=-2 * math.pi, in1=ang[:],
                                                         op0=ALU.mult, op1=ALU.add), reads=[b1], writes=[b1])
            t2 = self.sb(st, "t2w", [64, Q], F32)

            def wrap(shift, extra=()):
                S.op("dve", lambda e: e.tensor_scalar(t1[:], ang[:], shift, None, op0=ALU.add), reads=[b1] + list(extra), writes=[b1])
                S.op("dve", lambda e: e.tensor_scalar(t2[:], t1[:], math.pi, -2 * math.pi, op0=ALU.is_gt, op1=ALU.mult), reads=[b1], writes=[b1])
                S.op("dve", lambda e: e.tensor_tensor(out=t1[:], in0=t1[:], in1=t2[:], op=ALU.add), reads=[b1], writes=[b1])
                S.op("dve", lambda e: e.tensor_scalar(t2[:], t1[:], -math.pi, 2 * math.pi, op0=ALU.is_lt, op1=ALU.mult), reads=[b1], writes=[b1])
                S.op("dve", lambda e: e.tensor_tensor(out=t1[:], in0=t1[:], in1=t2[:], op=ALU.add), reads=[b1], writes=[b1])
            wrap(0.0)
            S.op("act", lambda e: e.activation(out=self.Etab[:, 0:Q], in_=t1[:], func=AF.Sin), reads=[b1], writes=[self.ebuf])
            wrap(math.pi / 2, [self.ebuf])
            S.op("act", lambda e: e.activation(out=self.Etab[:, Q:2 * Q], in_=t1[:], func=AF.Sin), reads=[b1], writes=[self.ebuf])

            ltmp = self.sb(st, "ltmp", [128, 4, H], F32)
            lbuf = Buf()
            for i, nm in enumerate(("lblP", "lblR")):
                for r in range(2):
                    self.load_fm(st, "lb%d%d" % (i, r), io[nm][r], AW, ltmp[:, 2 * i + r, :], lbuf)
            for i in range(2):
                S.op("dve", lambda e, i=i: e.tensor_tensor(out=ltmp[:, 2 * i, :], in0=ltmp[:, 2 * i, :], in1=ltmp[:, 2 * i + 1, :],
                                                          op=ALU.subtract), reads=[lbuf], writes=[lbuf])
                S.op("act", lambda e, i=i: e.activation(out=self.lb[:, 2 * i, :], in_=ltmp[:, 2 * i, :], func=AF.Sigmoid),
                     reads=[lbuf], writes=[self.cbuf])
                S.op("dve", lambda e, i=i: e.tensor_scalar(self.lb[:, 2 * i + 1, :], self.lb[:, 2 * i, :], -1.0, 1.0,
                                                          op0=ALU.mult, op1=ALU.add), reads=[self.cbuf], writes=[self.cbuf])
            self.load_fm(st, "gain", io["a_norm_g"], 128, self.gain[:, 0:1], self.cbuf)

            craw = self.sb(st, "craw", [128, 2, DC], F32)
            cT = self.sb(st, "cT", [128, DC, 2], BF16)
            cb = Buf()
            for r in range(2):
                self.load_fm(st, "c%d" % r, io["cvec"][r], D, craw[:, r, :], cb)
            for r in range(2):
                S.op("act", lambda e, r=r: e.activation(out=cT[:, :, r], in_=craw[:, r, :], func=AF.Silu), reads=[cb], writes=[cb])
            badaT = self.sb(st, "badaT", [128, 6 * DC], F32)
            bb = Buf()
            self.load_fm(st, "bada", io["b_ada"], 6 * D, badaT[:], bb)

            def epi(idx, tag, banks, NB):
                pt, pb = banks[0]
                S.op("dve", lambda e: e.tensor_tensor(out=self.modx[:, idx:idx + 1], in0=pt[:, 0:1], in1=badaT[:, idx:idx + 1],
                                                      op=ALU.add), reads=[pb, bb], writes=[self.mbuf])
                S.op("dve", lambda e: e.tensor_tensor(out=self.modc[:, idx:idx + 1], in0=pt[:, 1:2], in1=badaT[:, idx:idx + 1],
                                                      op=ALU.add), reads=[pb, bb], writes=[self.mbuf])
            chunks = [(io["w_ada"][:, oc * 128:(oc + 1) * 128], None) for oc in range(6 * DC)]
            ga = self.gemm_f_gen(st, "ada", cT, cb, DC, 2, chunks, epi, nslot=6)
            gr = self.phase_rows0_gen()
            na, nr = 6 * DC, cfg.nc_ + 2 * cfg.nt
            ia = 0
            for j in range(nr):
                tgt = ((j + 1) * na) // nr
                while ia < tgt:
                    next(ga, None)
                    ia += 1
                next(gr, None)
            for _ in ga:
                pass
            for _ in gr:
                pass
            for mt in (self.modx, self.modc):
                for m in (1, 4):
                    S.op("dve", lambda e, mt=mt, m=m: e.tensor_scalar(mt[:, m * DC:(m + 1) * DC], mt[:, m * DC:(m + 1) * DC], 1.0, None,
                                                                     op0=ALU.add), reads=[self.mbuf], writes=[self.mbuf])

    def ln_stats(self, xt, xb, stt, sb_, D):
        S = self.S
        nchunk = (D + 511) // 512
        for c in range(nchunk):
            S.op("dve", lambda e, c=c: e.bn_stats(stt["st"][:, c, :], xt[:, c * 512:min(D, (c + 1) * 512)]), reads=[xb], writes=[sb_])
        S.op("dve", lambda e: e.bn_aggr(stt["mv"][:], stt["st"][:, 0:nchunk, :]), reads=[sb_], writes=[sb_])
        S.op("act", lambda e: e.activation(out=stt["sd"][:], in_=stt["mv"][:, 1:2], func=AF.Sqrt, bias=stt["eps"][:, 0:1], scale=1.0),
             reads=[sb_, self.cbuf], writes=[sb_])
        S.op("dve", lambda e: e.reciprocal(stt["rstd"][:], stt["sd"][:]), reads=[sb_], writes=[sb_])
        S.op("dve", lambda e: e.scalar_tensor_tensor(out=stt["nmr"][:], in0=stt["mv"][:, 0:1], scalar=-1.0, in1=stt["rstd"][:],
                                                     op0=ALU.mult, op1=ALU.mult), reads=[sb_], writes=[sb_])

    def mk_stats(self, st, name):
        stt = {"st": self.sb(st, name + "_st", [128, 8, 6], F32), "mv": self.sb(st, name + "_mv", [128, 2], F32),
               "sd": self.sb(st, name + "_sd", [128, 1], F32), "rstd": self.sb(st, name + "_rs", [128, 1], F32),
               "nmr": self.sb(st, name + "_nm", [128, 1], F32), "eps": self.sb(st, name + "_ep", [128, 1], F32)}
        self.S.op("dve", lambda e: e.memset(stt["eps"][:], LN_EPS), writes=[self.cbuf])
        return stt

    def add_pos(self, xt, xb, n, res):
        cfg, S, io = self.cfg, self.S, self.io
        D = cfg.D
        HB = D // 2
        BWD = min(512, HB)
        sel = res["sel"][n % 2]
        sbf = res["selb"][n % 2]
        S.dma("sp", sel[:], io["rowsel"][n], writes=[sbf])
        k = 0
        for part in range(2):
            lhs = sel if part == 0 else res["csel"]
            for c0 in range(0, HB, BWD):
                pp, ppb = res["pp"][k % 2], res["ppb"][k % 2]
                k += 1
                S.op("pe", lambda e, lhs=lhs, c0=c0, pp=pp: e.matmul(pp[:, 0:BWD], lhsT=lhs[:], rhs=self.Etab[:, c0:c0 + BWD],
                                                                    start=True, stop=True),
                     reads=[sbf, res["cselb"], self.ebuf], writes=[ppb])
                o0 = part * HB + c0
                S.op("dve", lambda e, o0=o0, pp=pp: e.tensor_tensor(out=xt[:, o0:o0 + BWD], in0=xt[:, o0:o0 + BWD], in1=pp[:, 0:BWD],
                                                                   op=ALU.add), reads=[ppb, xb], writes=[xb])

    def mk_pos(self, st):
        res = {"sel": [self.sb(st, "sel%d" % i, [64, 128], F32) for i in range(2)], "selb": [Buf(), Buf()],
               "csel": self.sb(st, "csel", [64, 128], F32), "cselb": Buf(),
               "pp": [self.ps(st, "pp%d" % i, [128, 512], F32) for i in range(2)], "ppb": [Buf(), Buf()]}
        self.S.dma("sp", res["csel"][:], self.io["colsel"], writes=[res["cselb"]])
        return res

    def norm_T_store(self, xt, xb, stt, sb_, hn, hnb, hT, hTb, tp, tpb, mods, m_shift, m_scale, dst, t0):
        cfg, S = self.cfg, self.S
        D, DC = cfg.D, cfg.DC
        S.op("act", lambda e: e.activation(out=hn[:], in_=xt[:], func=AF.Identity, bias=stt["nmr"][:, 0:1], scale=stt["rstd"][:, 0:1]),
             reads=[xb, sb_], writes=[hnb])
        for dc in range(DC):
            p, pb = tp[dc % 2], tpb[dc % 2]
            S.op("pe", lambda e, dc=dc, p=p: e.transpose(p[:], hn[:, dc * 128:(dc + 1) * 128], self.ident_b[:]),
                 reads=[hnb, self.cbuf], writes=[pb])
            if mods is None:
                S.op("dve", lambda e, dc=dc, p=p: e.tensor_copy(hT[:, dc, :], p[:]), reads=[pb], writes=[hTb])
            else:
                S.op("dve", lambda e, dc=dc, p=p: e.tensor_scalar(hT[:, dc, :], p[:], mods[:, m_scale * DC + dc:m_scale * DC + dc + 1],
                                                                 mods[:, m_shift * DC + dc:m_shift * DC + dc + 1], op0=ALU.mult, op1=ALU.add),
                     reads=[pb, self.mbuf], writes=[hTb])
        S.dma("sp", dst[:, t0:t0 + 128].rearrange("(c p) t -> p c t", p=128), hT[:], reads=[hTb])

    def phase_rows0_gen(self):
        cfg, S, io, d = self.cfg, self.S, self.io, self.d
        D, DC, NT, CTX = cfg.D, cfg.DC, cfg.NT, cfg.CTX
        with ExitStack() as st:
            xts = [self.sb(st, "r0x%d" % i, [128, D], F32) for i in range(2)]
            hns = [self.sb(st, "r0h%d" % i, [128, D], BF16) for i in range(2)]
            hTs = [self.sb(st, "r0t%d" % i, [128, DC, 128], BF16) for i in range(2)]
            xbs, hbs, tbs = [Buf(), Buf()], [Buf(), Buf()], [Buf(), Buf()]
            tp = [self.ps(st, "r0p%d" % i, [128, 128], BF16) for i in range(2)]
            tpb = [Buf(), Buf()]
            stt = self.mk_stats(st, "r0")
            sb_ = Buf()
            pos = self.mk_pos(st)
            jobs = [("c", n) for n in range(cfg.nc_)] + [("x", n) for n in range(2 * cfg.nt)]
            for j, (kind, n) in enumerate(jobs):
                xt, xb, hn, hnb, hT, hTb = xts[j % 2], xbs[j % 2], hns[j % 2], hbs[j % 2], hTs[j % 2], tbs[j % 2]
                src = io["ctxl"] if kind == "c" else io["xloc"]
                S.dma("sp", xt[:], src[n * 128:(n + 1) * 128, :], writes=[xb])
                if kind == "x":
                    self.add_pos(xt, xb, n, pos)
                self.ln_stats(xt, xb, stt, sb_, D)
                self.norm_T_store(xt, xb, stt, sb_, hn, hnb, hT, hTb, tp, tpb, None, 0, 1,
                                  d["hTc"] if kind == "c" else d["hTx"], n * 128)
                yield j

    def phase_inproj(self):
        cfg, S, io, d = self.cfg, self.S, self.io, self.d
        D, DC, NT, CTX, H, G, AW, BW = cfg.D, cfg.DC, cfg.NT, cfg.CTX, cfg.H, cfg.G, cfg.AW, cfg.BW
        NX = 2 * NT

        def fam(name, n):
            if name == "zP":
                return [(io["w_zP"][:, c * 128:(c + 1) * 128], (name, c)) for c in range(n)]
            if name == "zR":
                return [(io["w_zR"][:, c * 128:(c + 1) * 128], (name, c)) for c in range(n)]
            o = cfg.off[name]
            return [(io["w_in"][:, o + c * 128:o + (c + 1) * 128], (name, c)) for c in range(n)]

        def run(setname, xsrc, T, fams, col0):
            with ExitStack() as st:
                xt, xb = self.load_xt(st, "ipx", xsrc, DC, T)
                mods = self.modc if setname == "ctx" else self.modx
                for dc in range(DC):
                    xq = xb[min(len(xb) - 1, dc // max(1, DC // len(xb)))]
                    S.op("dve", lambda e, dc=dc: e.tensor_scalar(xt[:, dc, :], xt[:, dc, :], mods[:, DC + dc:DC + dc + 1], mods[:, dc:dc + 1],
                                                                op0=ALU.mult, op1=ALU.add), reads=[xq, self.mbuf], writes=[xq])
                NBmax = min(T, 512)
                sf = [self.sb(st, "ipsf%d" % i, [128, T], F32) for i in range(2)]
                sh = [self.sb(st, "ipsh%d" % i, [128, T], BF16) for i in range(2)]
                sg = [self.sb(st, "ipsg%d" % i, [128, T], F32) for i in range(2)]
                sfb, shb, sgb = [Buf(), Buf()], [Buf(), Buf()], [Buf(), Buf()]

                def epi(idx, tag, banks, NB):
                    name, c = tag
                    i2 = idx % 2
                    rows = slice(c * 128, (c + 1) * 128)
                    for b, (pt, pb) in enumerate(banks):
                        cs = slice(b * NB, (b + 1) * NB)
                        if name in ("q", "g", "ga", "gb"):
                            fn = AF.Silu if name in ("q", "g") else AF.Sigmoid
                            S.op("act", lambda e, pt=pt, cs=cs, fn=fn: e.activation(out=sh[i2][:, cs], in_=pt[:, 0:NB], func=fn),
                                 reads=[pb], writes=[shb[i2]])
                        elif name in ("i", "u"):
                            S.op("act", lambda e, pt=pt, cs=cs: e.activation(out=sh[i2][:, cs], in_=pt[:, 0:NB], func=AF.Copy),
                                 reads=[pb], writes=[shb[i2]])
                        elif name == "v":
                            S.op("act", lambda e, pt=pt, cs=cs: e.activation(out=sf[i2][:, cs], in_=pt[:, 0:NB], func=AF.Copy),
                                 reads=[pb], writes=[sfb[i2]])
                        else:
                            li = 0 if name == "zP" else 2
                            S.op("act", lambda e, pt=pt, cs=cs: e.activation(out=sg[i2][:, cs], in_=pt[:, 0:NB], func=AF.Sigmoid),
                                 reads=[pb], writes=[sgb[i2]])
                            S.op("dve", lambda e, cs=cs, li=li: e.tensor_scalar(sg[i2][:, cs], sg[i2][:, cs], self.lb[:, li + 1, c:c + 1],
                                                                               self.lb[:, li, c:c + 1], op0=ALU.mult, op1=ALU.add),
                                 reads=[sgb[i2], self.cbuf], writes=[sgb[i2]])
                            S.op("act", lambda e, cs=cs: e.activation(out=sf[i2][:, cs], in_=sg[i2][:, cs], func=AF.Ln),
                                 reads=[sgb[i2]], writes=[sfb[i2]])
                            S.op("dve", lambda e, cs=cs: e.tensor_scalar(sh[i2][:, cs], sg[i2][:, cs], -1.0, 1.0, op0=ALU.mult, op1=ALU.add),
                                 reads=[sgb[i2]], writes=[shb[i2]])
                    if name == "q":
                        S.dma("sp", d["qsT"][rows, :], sh[i2][:], reads=[shb[i2]])
                    elif name == "g":
                        S.dma("sp", d["gsT"][rows, :], sh[i2][:], reads=[shb[i2]])
                    elif name == "ga":
                        S.dma("sp", d["sgaT"][rows, :], sh[i2][:], reads=[shb[i2]])
                    elif name == "gb":
                        S.dma("sp", d["sgbT"][rows, :], sh[i2][:], reads=[shb[i2]])
                    elif name == "u":
                        S.dma("sp", d["uT"][rows, :], sh[i2][:], reads=[shb[i2]])
                    elif name == "i":
                        S.dma("sp", d["iT"][rows, col0["i"]:col0["i"] + T], sh[i2][:], reads=[shb[i2]])
                    elif name == "v":
                        S.dma("sp", d["vT"][rows, :], sf[i2][:], reads=[sfb[i2]])
                    else:
                        lf, kk = ("lfP", "kP") if name == "zP" else ("lfR", "kR")
                        S.dma("sp", d[lf][rows, col0[name]:col0[name] + T], sf[i2][:], reads=[sfb[i2]])
                        S.dma("sp", d[kk][rows, col0[name]:col0[name] + T], sh[i2][:], reads=[shb[i2]])

                chunks = []
                for f in fams:
                    n = {"q": H, "i": H, "zP": H, "zR": H, "g": H, "u": G, "v": G, "ga": DC, "gb": DC}[f]
                    chunks += fam(f, n)
                self.gemm_f(st, "ip", xt, xb, DC, T, chunks, epi)
            S.barrier()

        run("ctx", d["hTc"], CTX, ["i", "zP", "zR"], {"i": 0, "zP": 0, "zR": 0})
        run("oth", d["hTx"][:, NT:NX], NT, ["i", "zR"], {"i": CTX + NT, "zR": CTX + NT})
        run("own", d["hTx"][:, 0:NT], NT, ["q", "i", "zP", "zR", "g", "u", "v", "ga", "gb"],
            {"i": CTX, "zP": CTX, "zR": CTX})

    def phase_scan(self):
        cfg, S, io, d = self.cfg, self.S, self.io, self.d
        NT, CTX, H = cfg.NT, cfg.CTX, cfg.H
        nt, nct = cfg.nt, cfg.nc_
        NX = 2 * NT
        NA = CTX + NX
        NP_ = CTX + NT
        QS = 128.0 ** -0.5
        for oc in range(cfg.DC):
            for g in range(cfg.DFF // 4096 if cfg.DFF >= 4096 else 1):
                kgs = min(4096, cfg.DFF)
                S.dma("pool", d["w2t"][oc * 128:(oc + 1) * 128, g * kgs:(g + 1) * kgs].rearrange("p (c f) -> p c f", f=128),
                      io["w_ff2"][g * kgs:(g + 1) * kgs, oc * 128:(oc + 1) * 128].rearrange("(c p) f -> p c f", p=128), writes=[self.w2tb])

        def vw(t, col0, dims):
            return bass.AP(t, col0, [[t.shape[1], 128]] + [list(x) for x in dims])

        with ExitStack() as st:
            qs = self.sb(st, "sc_qs", [128, NT], BF16)
            gs = self.sb(st, "sc_gs", [128, NT], BF16)
            iT = self.sb(st, "sc_iT", [128, NA], BF16)
            Lf = {"P": self.sb(st, "sc_lfP", [128, NP_], F32), "R": self.sb(st, "sc_lfR", [128, NA], F32)}
            Kk = {"P": self.sb(st, "sc_kP", [128, NP_], BF16), "R": self.sb(st, "sc_kR", [128, NA], BF16)}
            bq, bg, bi = Buf(), Buf(), Buf()
            bL = {"P": Buf(), "R": Buf()}
            bK = {"P": Buf(), "R": Buf()}
            ya = self.sb(st, "sc_ya", [128, NT], BF16)
            yab = Buf()
            rmask = self.sb(st, "sc_rmask", [128, NA], BF16)
            S.op("dve", lambda e: e.memset(rmask[:], 1.0), writes=[self.cbuf])
            S.op("dve", lambda e: e.memset(vw(rmask, 0, [[128, NA // 128], [1, 1]]), 0.0), writes=[self.cbuf])
            tB = self.sb(st, "sc_tB", [128, NA], F32)
            tE = self.sb(st, "sc_tE", [128, NA], F32)
            tD = self.sb(st, "sc_tD", [128, NT], F32)
            tE2 = self.sb(st, "sc_tE2", [128, NT], F32)
            bB, bE, bD, bE2 = Buf(), Buf(), Buf(), Buf()
            NTL = {"P": NP_ // 128, "R": NA // 128}
            b127 = {dd: self.sb(st, "sc_b127" + dd, [128, NTL[dd]], F32) for dd in "PR"}
            dl = {dd: self.sb(st, "sc_dl" + dd, [128, NTL[dd]], F32) for dd in "PR"}
            bdl = {"P": Buf(), "R": Buf()}
            Ks = {"P": self.sb(st, "sc_KsP", [128, NP_], BF16), "R": self.sb(st, "sc_KsR", [128, NA], BF16)}
            bKs = {"P": Buf(), "R": Buf()}
            Qi = {dd: self.sb(st, "sc_Qi" + dd, [128, NT], BF16) for dd in "PR"}
            Qn = {dd: self.sb(st, "sc_Qn" + dd, [128, NT], BF16) for dd in "PR"}
            Kn = {dd: self.sb(st, "sc_Kn" + dd, [128, NT], BF16) for dd in "PR"}
            Qc = {dd: self.sb(st, "sc_Qc" + dd, [128, NT // 2], BF16) for dd in "PR"}
            Kc = {dd: self.sb(st, "sc_Kc" + dd, [128, NT // 2], BF16) for dd in "PR"}
            bQ = {"P": Buf(), "R": Buf()}
            SscR = self.sb(st, "sc_SscR", [128, NT], BF16)
            bSR = Buf()
            Sst = {"P": self.sb(st, "sc_SP", [128, 128], F32), "R": self.sb(st, "sc_SR", [128, 128], F32)}
            Sb = {"P": Buf(), "R": Buf()}
            NW = 2
            mk = lambda nm, shp, dt: [self.sb(st, "sc_%s%d" % (nm, i), shp, dt) for i in range(NW)]
            KV = mk("KV", [64, 4, 128], BF16)
            Ssc = mk("Ssc", [128, 128], BF16)
            attP = mk("attP", [64, 2, 128], BF16)
            attR = mk("attR", [64, 2, 128], BF16)
            on = mk("on", [128, 128], BF16)
            sq = mk("sq", [128, 128], F32)
            sv = mk("sv", [128, 6], F32)
            wb = [Buf() for _ in range(NW)]
            for i in range(NW):
                S.op("dve", lambda e, i=i: e.memset(attP[i][:], 0.0), writes=[wb[i]])
                S.op("dve", lambda e, i=i: e.memset(attR[i][:], 0.0), writes=[wb[i]])
            ptb = [self.ps(st, "sc_pt%d" % i, [64, 4, 128], BF16) for i in range(2)]
            ptbb = [Buf(), Buf()]
            pU0 = self.ps(st, "sc_pU", [128, 128], F32)
            pU = [pU0, pU0]
            pUb0 = Buf()
            pUb = [pUb0, pUb0]
            pA = [self.ps(st, "sc_pA%d" % i, [64, 4, 128], F32) for i in range(2)]
            pAb = [Buf(), Buf()]
            pO = [self.ps(st, "sc_pO%d" % i, [128, 128], F32) for i in range(2)]
            pOb = [Buf(), Buf()]
            pY = self.ps(st, "sc_pY", [128, 128], BF16)
            pYb = Buf()
            cnt = [0]
            mP, mR = self.maskP[0:64, 0:64], self.maskR[0:64, 0:64]

            def prep_ops(dd):
                N, ntl = (NP_, NTL["P"]) if dd == "P" else (NA, NTL["R"])
                L, kk, o0 = Lf[dd], Kk[dd], CTX
                sgn, ia, ic = (1.0, 31, 63) if dd == "P" else (-1.0, 32, 64)
                full3 = lambda t, c0=0, n=ntl: vw(t, c0, [[128, n], [1, 128]])
                own_h = lambda t, c0: vw(t, c0, [[64, 2 * nt], [1, 64]])
                half = lambda t, c0, hoff: vw(t, c0 + hoff, [[128, nt], [1, 64]])
                cmp_ = lambda t: vw(t, 0, [[64, nt], [1, 64]])
                ops = []
                A = ops.append
                A(lambda: S.op("dve", lambda e: e.tensor_tensor_scan(out=tB[:, 0:N], data0=rmask[:, 0:N], data1=L[:, 0:N], initial=0.0,
                                                                   op0=ALU.mult, op1=ALU.add), reads=[bL[dd], self.cbuf], writes=[bB]))
                A(lambda: S.op("dve", lambda e: e.tensor_copy(b127[dd][:], vw(tB, 127, [[128, ntl]])), reads=[bB], writes=[bdl[dd]]))
                A(lambda: S.op("act", lambda e: e.activation(out=dl[dd][:], in_=b127[dd][:], func=AF.Exp), reads=[bdl[dd]], writes=[bdl[dd]]))
                if dd == "R":
                    A(lambda: S.op("dve", lambda e: e.tensor_tensor(out=tB[:, 0:N], in0=tB[:, 0:N], in1=L[:, 0:N], op=ALU.subtract),
                                   reads=[bB, bL[dd]], writes=[bB]))
                    A(lambda: S.op("act", lambda e: e.activation(out=tE[:, 0:N], in_=tB[:, 0:N], func=AF.Exp), reads=[bB], writes=[bE]))
                else:
                    A(lambda: S.op("dve", lambda e: e.tensor_tensor(out=full3(tE), in0=vw(b127[dd], 0, [[1, ntl], [0, 128]]), in1=full3(tB),
                                                                    op=ALU.subtract), reads=[bB, bdl[dd]], writes=[bE]))
                    A(lambda: S.op("act", lambda e: e.activation(out=tE[:, 0:N], in_=tE[:, 0:N], func=AF.Exp), reads=[bE], writes=[bE]))
                A(lambda: S.op("dve", lambda e: e.tensor_tensor(out=Ks[dd][:, 0:N], in0=kk[:, 0:N], in1=tE[:, 0:N], op=ALU.mult),
                               reads=[bE, bK[dd]], writes=[bKs[dd]]))
                if dd == "P":
                    A(lambda: S.op("act", lambda e: e.activation(out=tE2[:], in_=tB[:, o0:o0 + NT], func=AF.Exp), reads=[bB], writes=[bE2]))
                else:
                    A(lambda: S.op("dve", lambda e: e.tensor_tensor(out=full3(tD, 0, nt), in0=vw(b127[dd], nct, [[1, nt], [0, 128]]),
                                                                    in1=full3(tB, o0, nt), op=ALU.subtract), reads=[bB, bdl[dd]], writes=[bD]))
                    A(lambda: S.op("act", lambda e: e.activation(out=tE2[:], in_=tD[:], func=AF.Exp), reads=[bD], writes=[bE2]))
                A(lambda: S.op("dve", lambda e: e.tensor_tensor(out=Qi[dd][:], in0=qs[:], in1=tE2[:], op=ALU.mult), reads=[bE2, bq], writes=[bQ[dd]]))
                A(lambda: S.op("dve", lambda e: e.tensor_tensor(out=own_h(tD, 0), in0=own_h(tB, o0), in1=vw(tB, o0 + ia, [[64, 2 * nt], [0, 64]]),
                                                                op=ALU.subtract), reads=[bB, bE2], writes=[bD]))
                A(lambda: S.op("act", lambda e: e.activation(out=tE2[:], in_=tD[:], func=AF.Exp, scale=sgn), reads=[bD, bQ[dd]], writes=[bE2]))
                A(lambda: S.op("dve", lambda e: e.tensor_tensor(out=Qn[dd][:], in0=qs[:], in1=tE2[:], op=ALU.mult), reads=[bE2, bq], writes=[bQ[dd]]))
                A(lambda: S.op("act", lambda e: e.activation(out=tE2[:], in_=tD[:], func=AF.Exp, scale=-sgn), reads=[bD, bQ[dd]], writes=[bE2]))
                A(lambda: S.op("dve", lambda e: e.tensor_tensor(out=Kn[dd][:], in0=kk[:, o0:o0 + NT], in1=tE2[:], op=ALU.mult),
                               reads=[bE2, bK[dd]], writes=[bQ[dd]]))
                A(lambda: S.op("dve", lambda e: e.tensor_tensor(out=full3(tD, 0, nt), in0=full3(tB, o0, nt), in1=vw(tB, o0 + ic, [[128, nt], [0, 128]]),
                                                                op=ALU.subtract), reads=[bB, bE2], writes=[bD]))
                qh, kh = (64, 0) if dd == "P" else (0, 64)
                A(lambda: S.op("act", lambda e: e.activation(out=cmp_(tE2), in_=half(tD, 0, qh), func=AF.Exp, scale=sgn), reads=[bD, bQ[dd]], writes=[bE2]))
                A(lambda: S.op("dve", lambda e: e.tensor_tensor(out=cmp_(Qc[dd]), in0=half(qs, 0, qh), in1=cmp_(tE2), op=ALU.mult),
                               reads=[bE2, bq], writes=[bQ[dd]]))
                A(lambda: S.op("act", lambda e: e.activation(out=cmp_(tE2), in_=half(tD, 0, kh), func=AF.Exp, scale=-sgn), reads=[bD, bQ[dd]], writes=[bE2]))
                A(lambda: S.op("dve", lambda e: e.tensor_tensor(out=cmp_(Kc[dd]), in0=half(kk, o0, kh), in1=cmp_(tE2), op=ALU.mult),
                               reads=[bE2, bK[dd]], writes=[bQ[dd]]))
                return ops

            def st_T(dd, col, slot):
                w = wb[slot]
                k2 = slot % 2
                for hh in range(2):
                    S.op("pe", lambda e, hh=hh: e.transpose(ptb[k2][:, hh, :], Ks[dd][:, col + hh * 64:col + (hh + 1) * 64], self.ident_b[:]),
                         reads=[bKs[dd], self.cbuf], writes=[ptbb[k2]])
                    S.op("pe", lambda e, hh=hh: e.transpose(ptb[k2][:, 2 + hh, :], iT[:, col + hh * 64:col + (hh + 1) * 64], self.ident_b[:]),
                         reads=[bi, self.cbuf], writes=[ptbb[k2]])
                S.op("act", lambda e: e.activation(out=KV[slot][:], in_=ptb[k2][:], func=AF.Copy), reads=[ptbb[k2]], writes=[w])

            def st_U(dd, til, slot):
                w = wb[slot]
                k2 = slot % 2
                for hh in range(2):
                    S.op("pe", lambda e, hh=hh: e.matmul(pU[k2][:], lhsT=KV[slot][:, hh, :], rhs=KV[slot][:, 2 + hh, :], start=(hh == 0), stop=(hh == 1)),
                         reads=[w], writes=[pUb[k2]])
                S.op("dve", lambda e: e.scalar_tensor_tensor(out=Sst[dd][:], in0=Sst[dd][:], scalar=dl[dd][:, til:til + 1], in1=pU[k2][:],
                                                             op0=ALU.mult, op1=ALU.add), reads=[pUb[k2], bdl[dd], Sb[dd]], writes=[Sb[dd]])

            for h in range(H):
                rows = slice(h * 128, (h + 1) * 128)
                for (dst_, src_, bf) in ((Lf["R"], d["lfR"], bL["R"]), (Kk["R"], d["kR"], bK["R"]), (iT, d["iT"], bi), (qs, d["qsT"], bq),
                                         (Lf["P"], d["lfP"], bL["P"]), (Kk["P"], d["kP"], bK["P"]), (gs, d["gsT"], bg)):
                    S.dma("sp", dst_[:], src_[rows, :], writes=[bf])
                for dd in ("P", "R"):
                    S.op("dve", lambda e, dd=dd: e.memset(Sst[dd][:], 0.0), writes=[Sb[dd]])
                for f in prep_ops("R"):
                    f()
                pops = prep_ops("P")
                seq = [("c", n) for n in reversed(range(nct))] + [("x", n) for n in reversed(range(nt, 2 * nt))] + \
                      [("x", n) for n in reversed(range(nt))]
                tiles = [(n if kind == "c" else nct + n) for (kind, n) in seq]
                st_T("R", tiles[0] * 128, 0)
                for j, (kind, n) in enumerate(seq):
                    slot = j % NW
                    til = tiles[j]
                    if j + 1 < len(seq):
                        st_T("R", tiles[j + 1] * 128, (j + 1) % NW)
                    if kind == "x" and n < nt:
                        oc = slice(n * 128, (n + 1) * 128)
                        S.op("act", lambda e, oc=oc: e.activation(out=SscR[:, oc], in_=Sst["R"][:], func=AF.Copy), reads=[Sb["R"]], writes=[bSR])
                    st_U("R", til, slot)
                    if pops:
                        pops.pop(0)()
                while pops:
                    pops.pop(0)()
                st_T("P", 0, 0)
                for n in range(nct):
                    st_T("P", (n + 1) * 128, (n + 1) % NW)
                    st_U("P", n, n % NW)
                for n in range(nt):
                    slot = (nct + n) % NW
                    w = wb[slot]
                    oc = slice(n * 128, (n + 1) * 128)
                    och = slice(n * 64, (n + 1) * 64)
                    ocA, ocB = slice(n * 128, n * 128 + 64), slice(n * 128 + 64, (n + 1) * 128)
                    til = nct + n
                    col = til * 128
                    k2 = slot % 2
                    S.op("act", lambda e, slot=slot: e.activation(out=Ssc[slot][:], in_=Sst["P"][:], func=AF.Copy), reads=[Sb["P"]], writes=[w])
                    pa = pA[k2]
                    blocks = [("P", 0, slice(0, 64), Kn, Qn, ocA), ("P", 0, slice(64, 128), Kc, Qc, och), ("P", 1, slice(64, 128), Kn, Qn, ocB),
                              ("R", 2, slice(0, 64), Kn, Qn, ocA), ("R", 3, slice(64, 128), Kn, Qn, ocB), ("R", 3, slice(0, 64), Kc, Qc, och)]
                    for (dd, row, cs, KK, QQ, sl) in blocks:
                        S.op("pe", lambda e, dd=dd, row=row, cs=cs, KK=KK, QQ=QQ, sl=sl, pa=pa: e.matmul(
                            pa[:, row, cs], lhsT=KK[dd][:, sl], rhs=QQ[dd][:, sl], start=True, stop=True), reads=[bQ[dd]], writes=[pAb[k2]])
                    aP, aR = attP[slot], attR[slot]
                    S.op("dve", lambda e, aP=aP, pa=pa: e.copy_predicated(aP[:, 0, 0:64], mP, pa[:, 0, 0:64]), reads=[pAb[k2], self.cbuf], writes=[w])
                    S.op("act", lambda e, aP=aP, pa=pa: e.activation(out=aP[:, 0, 64:128], in_=pa[:, 0, 64:128], func=AF.Copy), reads=[pAb[k2]], writes=[w])
                    S.op("dve", lambda e, aP=aP, pa=pa: e.copy_predicated(aP[:, 1, 64:128], mP, pa[:, 1, 64:128]), reads=[pAb[k2], self.cbuf], writes=[w])
                    S.op("dve", lambda e, aR=aR, pa=pa: e.copy_predicated(aR[:, 0, 0:64], mR, pa[:, 2, 0:64]), reads=[pAb[k2], self.cbuf], writes=[w])
                    S.op("act", lambda e, aR=aR, pa=pa: e.activation(out=aR[:, 1, 0:64], in_=pa[:, 3, 0:64], func=AF.Copy), reads=[pAb[k2]], writes=[w])
                    S.op("dve", lambda e, aR=aR, pa=pa: e.copy_predicated(aR[:, 1, 64:128], mR, pa[:, 3, 64:128]), reads=[pAb[k2], self.cbuf], writes=[w])
                    if n + 1 < nt:
                        st_T("P", (til + 1) * 128, (til + 1) % NW)
                    st_U("P", til, slot)
                    po = pO[k2]
                    kv = KV[slot]
                    mm = [(aP[:, 0, :], kv[:, 2, :], [w]), (aP[:, 1, :], kv[:, 3, :], [w]), (Qi["P"][:, oc], Ssc[slot][:], [w, bQ["P"]]),
                          (aR[:, 0, :], kv[:, 2, :], [w]), (aR[:, 1, :], kv[:, 3, :], [w]), (Qi["R"][:, oc], SscR[:, oc], [bQ["R"], bSR])]
                    for j, (l_, r_, rd) in enumerate(mm):
                        S.op("pe", lambda e, l_=l_, r_=r_, j=j, po=po: e.matmul(po[:], lhsT=l_, rhs=r_, start=(j == 0), stop=(j == len(mm) - 1)),
                             reads=rd, writes=[pOb[k2]])
                    s4 = sv[slot]
                    S.op("act", lambda e, slot=slot, po=po, s4=s4: e.activation(out=sq[slot][:], in_=po[:], func=AF.Square, accum_out=s4[:, 0:1]),
                         reads=[pOb[k2], w], writes=[w])
                    S.op("dve", lambda e, s4=s4: e.tensor_scalar(s4[:, 1:2], s4[:, 0:1], QS * QS / 128.0, LN_EPS, op0=ALU.mult, op1=ALU.add), reads=[w], writes=[w])
                    S.op("act", lambda e, s4=s4: e.activation(out=s4[:, 2:3], in_=s4[:, 1:2], func=AF.Sqrt), reads=[w], writes=[w])
                    S.op("dve", lambda e, s4=s4: e.reciprocal(s4[:, 3:4], s4[:, 2:3]), reads=[w], writes=[w])
                    S.op("dve", lambda e, s4=s4: e.tensor_scalar(s4[:, 4:5], s4[:, 3:4], QS, None, op0=ALU.mult), reads=[w], writes=[w])
                    S.op("act", lambda e, slot=slot, po=po, s4=s4: e.activation(out=on[slot][:], in_=po[:], func=AF.Identity, scale=s4[:, 4:5]),
                         reads=[pOb[k2], w], writes=[w])
                    S.op("pe", lambda e, slot=slot: e.transpose(pY[:], on[slot][:], self.ident_b[:]), reads=[w, self.cbuf], writes=[pYb])
                    S.op("dve", lambda e, oc=oc: e.scalar_tensor_tensor(out=ya[:, oc], in0=pY[:], scalar=self.gain[:, 0:1], in1=gs[:, oc],
                                                                        op0=ALU.mult, op1=ALU.mult), reads=[pYb, bg, self.cbuf], writes=[yab])
                S.dma("sp", d["yaT"][rows, :], ya[:], reads=[yab])

    def phase_mixb(self):
        cfg, S, io, d = self.cfg, self.S, self.io, self.d
        NT, G, BW = cfg.NT, cfg.G, cfg.BW
        with ExitStack() as st:
            wsf = self.sb(st, "mb_wsf", [128, G, 128], F32)
            ws = self.sb(st, "mb_ws", [128, G, 128], BF16)
            bsf = self.sb(st, "mb_bsf", [1, G * 128], F32)
            bs = self.sb(st, "mb_bs", [1, G * 128], BF16)
            onesr = self.sb(st, "mb_ones", [1, 128], BF16)
            gB = self.sb(st, "mb_gB", [128, BW], F32)
            bB = self.sb(st, "mb_bB", [128, BW], F32)
            cb = Buf()
            S.dma("sp", wsf[:], io["w_sT"].rearrange("g s t -> s g t"), writes=[cb])
            S.dma("sp", bsf[:], io["b_s"].rearrange("g t -> (g t)").rearrange("(o n) -> o n", o=1), writes=[cb])
            S.dma("sp", gB[:], bcast_rows(io["v_norm_g"], 128, BW), writes=[cb])
            S.dma("sp", bB[:], bcast_rows(io["v_norm_b"], 128, BW), writes=[cb])
            S.op("dve", lambda e: e.tensor_copy(ws[:], wsf[:]), reads=[cb], writes=[cb])
            S.op("dve", lambda e: e.tensor_copy(bs[:], bsf[:]), reads=[cb], writes=[cb])
            S.op("dve", lambda e: e.memset(onesr[:], 1.0), writes=[cb])
            vT = [self.sb(st, "mb_vT%d" % i, [128, G, 128], F32) for i in range(2)]
            uT = [self.sb(st, "mb_uT%d" % i, [128, G, 128], BF16) for i in range(2)]
            vtm = [self.sb(st, "mb_vtm%d" % i, [128, BW], F32) for i in range(2)]
            vn = [self.sb(st, "mb_vn%d" % i, [128, BW], BF16) for i in range(2)]
            yb = [self.sb(st, "mb_yb%d" % i, [128, G, 128], BF16) for i in range(2)]
            lb_, vb, nb_, ybb = [Buf(), Buf()], [Buf(), Buf()], [Buf(), Buf()], [Buf(), Buf()]
            pt = [self.ps(st, "mb_pt%d" % i, [128, 128], F32) for i in range(2)]
            ptb = [Buf(), Buf()]
            pm = [self.ps(st, "mb_pm%d" % i, [128, 128], F32) for i in range(2)]
            pmb = [Buf(), Buf()]
            stt = self.mk_stats(st, "mb")
            sb_ = Buf()
            for n in range(cfg.nt):
                i2 = n % 2
                ts_ = slice(n * 128, (n + 1) * 128)
                S.dma("sp", vT[i2][:], d["vT"][:, ts_].rearrange("(g p) t -> p g t", p=128), writes=[lb_[i2]])
                S.dma("sp", uT[i2][:], d["uT"][:, ts_].rearrange("(g p) t -> p g t", p=128), writes=[lb_[i2]])
                for g in range(G):
                    k = g % 2
                    S.op("pe", lambda e, g=g, k=k: e.transpose(pt[k][:], vT[i2][:, g, :], self.ident_f[:]), reads=[lb_[i2], self.cbuf], writes=[ptb[k]])
                    S.op("act", lambda e, g=g, k=k: e.activation(out=vtm[i2][:, g * 128:(g + 1) * 128], in_=pt[k][:], func=AF.Copy),
                         reads=[ptb[k]], writes=[vb[i2]])
                self.ln_stats(vtm[i2], vb[i2], stt, sb_, BW)
                S.op("dve", lambda e: e.tensor_scalar(vtm[i2][:], vtm[i2][:], stt["rstd"][:, 0:1], stt["nmr"][:, 0:1], op0=ALU.mult, op1=ALU.add),
                     reads=[sb_, vb[i2]], writes=[vb[i2]])
                S.op("dve", lambda e: e.tensor_tensor(out=vtm[i2][:], in0=vtm[i2][:], in1=gB[:], op=ALU.mult), reads=[vb[i2], cb], writes=[vb[i2]])
                S.op("dve", lambda e: e.tensor_tensor(out=vn[i2][:], in0=vtm[i2][:], in1=bB[:], op=ALU.add), reads=[vb[i2], cb], writes=[nb_[i2]])
                for g in range(G):
                    k = g % 2
                    S.op("pe", lambda e, g=g, k=k: e.matmul(pm[k][:], lhsT=vn[i2][:, g * 128:(g + 1) * 128], rhs=ws[:, g, :], start=True, stop=False),
                         reads=[nb_[i2], cb], writes=[pmb[k]])
                    S.op("pe", lambda e, g=g, k=k: e.matmul(pm[k][:], lhsT=onesr[:], rhs=bs[:, g * 128:(g + 1) * 128], start=False, stop=True),
                         reads=[cb], writes=[pmb[k]])
                    S.op("dve", lambda e, g=g, k=k: e.tensor_tensor(out=yb[i2][:, g, :], in0=pm[k][:], in1=uT[i2][:, g, :], op=ALU.mult),
                         reads=[pmb[k], lb_[i2]], writes=[ybb[i2]])
                S.dma("sp", d["ybT"][:, ts_].rearrange("(g p) t -> p g t", p=128), yb[i2][:], reads=[ybb[i2]])

    def phase_merge(self):
        cfg, S, io, d = self.cfg, self.S, self.io, self.d
        D, DC, NT, H, G = cfg.D, cfg.DC, cfg.NT, cfg.H, cfg.G
        for which in ("a", "b"):
            with ExitStack() as st:
                KC = H if which == "a" else G
                xt, xb = self.load_xt(st, "mgx", d["yaT"] if which == "a" else d["ybT"], KC, NT)
                W = io["w_proj_a"] if which == "a" else io["w_proj_b"]
                sg = [self.sb(st, "mg_sg%d" % i, [128, NT], BF16) for i in range(2)]
                ma = [self.sb(st, "mg_ma%d" % i, [128, NT], F32) for i in range(2)]
                mo = [self.sb(st, "mg_mo%d" % i, [128, NT], BF16) for i in range(2)]
                sgb, mab, mob = [Buf(), Buf()], [Buf(), Buf()], [Buf(), Buf()]
                ma2 = [self.sb(st, "mg_m2%d" % i, [128, NT], F32) for i in range(2)]
                ma2b = [Buf(), Buf()]

                def pre(idx, tag, which=which):
                    i2 = idx % 2
                    rows = slice(idx * 128, (idx + 1) * 128)
                    S.dma("sp", sg[i2][:], d["sgaT" if which == "a" else "sgbT"][rows, :], writes=[sgb[i2]])
                    if which == "b":
                        S.dma("sp", ma[i2][:], d["mA"][rows, :], writes=[mab[i2]])

                def epi(idx, tag, banks, NB, which=which):
                    i2 = idx % 2
                    rows = slice(idx * 128, (idx + 1) * 128)
                    for b, (pt, pb) in enumerate(banks):
                        cs = slice(b * NB, (b + 1) * NB)
                        if which == "a":
                            S.op("dve", lambda e, pt=pt, cs=cs: e.tensor_tensor(out=ma[i2][:, cs], in0=pt[:, 0:NB], in1=sg[i2][:, cs], op=ALU.mult),
                                 reads=[pb, sgb[i2]], writes=[mab[i2]])
                        else:
                            S.op("dve", lambda e, pt=pt, cs=cs: e.tensor_tensor(out=ma2[i2][:, cs], in0=pt[:, 0:NB], in1=sg[i2][:, cs], op=ALU.mult),
                                 reads=[pb, sgb[i2]], writes=[ma2b[i2]])
                            S.op("dve", lambda e, cs=cs: e.tensor_tensor(out=mo[i2][:, cs], in0=ma[i2][:, cs], in1=ma2[i2][:, cs], op=ALU.add),
                                 reads=[ma2b[i2], mab[i2]], writes=[mob[i2]])
                    if which == "a":
                        S.dma("sp", d["mA"][rows, :], ma[i2][:], reads=[mab[i2]])
                    else:
                        S.dma("sp", d["mergedT"][rows, :], mo[i2][:], reads=[mob[i2]])
                chunks = [(W[:, oc * 128:(oc + 1) * 128], None) for oc in range(DC)]
                self.gemm_f(st, "mg", xt, xb, KC, NT, chunks, epi, pre=pre, nslot=5)
            S.barrier()

    def gemm_to_rows(self, name, xsrc, KC, W, mod_idx, dst, TBLK):
        cfg, S = self.cfg, self.S
        DC, NT = cfg.DC, cfg.NT
        with ExitStack() as st:
            xt = self.sb(st, name + "x", [128, KC, TBLK], BF16)
            xbs = [Buf() for _ in range(4 if KC >= 4 else 1)]
            res = self.gemm_alloc(st, name, KC, TBLK, 2, nslot=6)
            NBm = min(TBLK, 512)
            nj = NBm // 128
            sf = [self.sb(st, name + "sf%d" % i, [128, NBm], F32) for i in range(2)]
            tt = [self.sb(st, name + "tt%d" % i, [128, nj, 128], F32) for i in range(2)]
            sfb, ttb = [Buf(), Buf()], [Buf(), Buf()]
            ptp = [self.ps(st, name + "pp%d" % i, [128, nj, 128], F32) for i in range(2)]
            ptb = [Buf(), Buf()]
            k = [0]
            for t0 in range(0, NT, TBLK):
                self.load_xt(st, name + "x", xsrc[:, t0:t0 + TBLK], KC, TBLK, xt=xt, xbs=xbs)

                def epi(idx, tag, banks, NB, t0=t0):
                    for b, (pt, pb) in enumerate(banks):
                        i2 = k[0] % 2
                        k[0] += 1
                        S.op("act", lambda e, pt=pt, i2=i2: e.activation(out=sf[i2][:, 0:NB], in_=pt[:, 0:NB], func=AF.Identity,
                                                                         scale=self.modx[:, mod_idx * DC + idx:mod_idx * DC + idx + 1]),
                             reads=[pb, self.mbuf], writes=[sfb[i2]])
                        for j in range(NB // 128):
                            S.op("pe", lambda e, j=j, i2=i2: e.transpose(ptp[i2][:, j, :], sf[i2][:, j * 128:(j + 1) * 128], self.ident_f[:]),
                                 reads=[sfb[i2], self.cbuf], writes=[ptb[i2]])
                        S.op("dve", lambda e, i2=i2: e.tensor_copy(tt[i2][:, 0:NB // 128, :], ptp[i2][:, 0:NB // 128, :]), reads=[ptb[i2]], writes=[ttb[i2]])
                        r0 = t0 + b * NB
                        S.dma("sp", dst[r0:r0 + NB, idx * 128:(idx + 1) * 128].rearrange("(j p) c -> p j c", p=128), tt[i2][:, 0:NB // 128, :],
                              reads=[ttb[i2]])
                if isinstance(W, tuple):
                    chunks = [(("tiled", W[1][oc * 128:(oc + 1) * 128, :], W[2]), None) for oc in range(DC)]
                else:
                    chunks = [(W[:, oc * 128:(oc + 1) * 128], None) for oc in range(DC)]
                self.gemm_f(st, name, xt, xbs, KC, TBLK, chunks, epi, nsets=2, res=res)
        S.barrier()

    def phase_wout(self):
        cfg = self.cfg
        self.gemm_to_rows("wo", self.d["mergedT"], cfg.DC, self.io["w_out"], 2, self.d["x1pre"], min(cfg.NT, 1024))

    def phase_rows1(self):
        cfg, S, io, d = self.cfg, self.S, self.io, self.d
        D, DC, NT = cfg.D, cfg.DC, cfg.NT
        with ExitStack() as st:
            xts = [self.sb(st, "r1x%d" % i, [128, D], F32) for i in range(2)]
            pre = [self.sb(st, "r1p%d" % i, [128, D], F32) for i in range(2)]
            hn = self.sb(st, "r1h", [128, D], BF16)
            hT = self.sb(st, "r1t", [128, DC, 128], BF16)
            gB = self.sb(st, "r1g", [128, D], F32)
            bB = self.sb(st, "r1b", [128, D], F32)
            cb = Buf()
            S.dma("sp", gB[:], bcast_rows(io["ln1_g"], 128, D), writes=[cb])
            S.dma("sp", bB[:], bcast_rows(io["ln1_b"], 128, D), writes=[cb])
            xbs, pbs, hnb, hTb = [Buf(), Buf()], [Buf(), Buf()], Buf(), Buf()
            tp = [self.ps(st, "r1tp%d" % i, [128, 128], BF16) for i in range(2)]
            tpb = [Buf(), Buf()]
            stt = self.mk_stats(st, "r1")
            sb_ = Buf()
            pos = self.mk_pos(st)
            for n in range(cfg.nt):
                i2 = n % 2
                xt, xb, pr, prb = xts[i2], xbs[i2], pre[i2], pbs[i2]
                rs = slice(n * 128, (n + 1) * 128)
                S.dma("sp", xt[:], io["xloc"][rs, :], writes=[xb])
                S.dma("sp", pr[:], d["x1pre"][rs, :], writes=[prb])
                self.add_pos(xt, xb, n, pos)
                S.op("dve", lambda e, xt=xt, pr=pr: e.scalar_tensor_tensor(out=pr[:], in0=xt[:], scalar=cfg.ALPHA, in1=pr[:], op0=ALU.mult, op1=ALU.add),
                     reads=[xb, prb], writes=[prb])
                self.ln_stats(pr, prb, stt, sb_, D)
                S.op("act", lambda e, pr=pr: e.activation(out=pr[:], in_=pr[:], func=AF.Identity, bias=stt["nmr"][:, 0:1], scale=stt["rstd"][:, 0:1]),
                     reads=[sb_, prb], writes=[prb])
                S.op("dve", lambda e, pr=pr: e.tensor_tensor(out=pr[:], in0=pr[:], in1=gB[:], op=ALU.mult), reads=[prb, cb], writes=[prb])
                S.op("pool", lambda e, pr=pr: e.tensor_tensor(out=pr[:], in0=pr[:], in1=bB[:], op=ALU.add), reads=[prb, cb], writes=[prb])
                S.dma("sp", d["x1"][rs, :], pr[:], reads=[prb])
                self.ln_stats(pr, prb, stt, sb_, D)
                self.norm_T_store(pr, prb, stt, sb_, hn, hnb, hT, hTb, tp, tpb, self.modx, 3, 4, d["h2T"], n * 128)

    def phase_ff(self):
        cfg, S, io, d = self.cfg, self.S, self.io, self.d
        D, DC, NT, DFF = cfg.D, cfg.DC, cfg.NT, cfg.DFF
        with ExitStack() as st:
            xt, xb = self.load_xt(st, "f1x", d["h2T"], DC, NT)
            r_ = [self.sb(st, "f1r%d" % i, [128, NT], F32) for i in range(2)]
            hh = [self.sb(st, "f1h%d" % i, [128, NT], BF16) for i in range(2)]
            rb, hb = [Buf(), Buf()], [Buf(), Buf()]

            def epi(idx, tag, banks, NB):
                i2 = idx % 2
                for b, (pt, pb) in enumerate(banks):
                    cs = slice(b * NB, (b + 1) * NB)
                    S.op("act", lambda e, pt=pt, cs=cs: e.activation(out=r_[i2][:, cs], in_=pt[:, 0:NB], func=AF.Relu), reads=[pb], writes=[rb[i2]])
                    S.op("dve", lambda e, pt=pt, cs=cs: e.tensor_tensor(out=hh[i2][:, cs], in0=pt[:, 0:NB], in1=r_[i2][:, cs], op=ALU.mult),
                         reads=[pb, rb[i2]], writes=[hb[i2]])
                S.dma("sp", d["hidT"][idx * 128:(idx + 1) * 128, :], hh[i2][:], reads=[hb[i2]])
            chunks = [(io["w_ff1"][:, oc * 128:(oc + 1) * 128], None) for oc in range(DFF // 128)]
            self.gemm_f(st, "f1", xt, xb, DC, NT, chunks, epi)
        S.barrier()
        self.gemm_to_rows("f2", d["hidT"], DFF // 128, ("tiled", d["w2t"], self.w2tb), 5, d["x2pre"], min(NT, 512))

    def phase_rows2(self):
        cfg, S, io, d = self.cfg, self.S, self.io, self.d
        D, NT = cfg.D, cfg.NT
        NBF = 3
        with ExitStack() as st:
            x1s = [self.sb(st, "r2x%d" % i, [128, D], F32) for i in range(NBF)]
            pre = [self.sb(st, "r2p%d" % i, [128, D], F32) for i in range(NBF)]
            gB = self.sb(st, "r2g", [128, D], F32)
            bB = self.sb(st, "r2b", [128, D], F32)
            cb = Buf()
            S.dma("sp", gB[:], bcast_rows(io["ln2_g"], 128, D), writes=[cb])
            S.dma("sp", bB[:], bcast_rows(io["ln2_b"], 128, D), writes=[cb])
            xbs, pbs = [Buf() for _ in range(NBF)], [Buf() for _ in range(NBF)]
            stts = [self.mk_stats(st, "r2_%d" % i) for i in range(NBF)]
            sbs = [Buf() for _ in range(NBF)]
            for n in range(cfg.nt):
                i2 = n % NBF
                xt, xb, pr, prb, stt, sb_ = x1s[i2], xbs[i2], pre[i2], pbs[i2], stts[i2], sbs[i2]
                rs = slice(n * 128, (n + 1) * 128)
                S.dma("sp", xt[:], d["x1"][rs, :], writes=[xb])
                S.dma("sp", pr[:], d["x2pre"][rs, :], writes=[prb])
                S.op("dve", lambda e, xt=xt, pr=pr: e.scalar_tensor_tensor(out=pr[:], in0=xt[:], scalar=cfg.ALPHA, in1=pr[:], op0=ALU.mult, op1=ALU.add),
                     reads=[xb, prb], writes=[prb])
                self.ln_stats(pr, prb, stt, sb_, D)
                S.op("act", lambda e, pr=pr, stt=stt: e.activation(out=pr[:], in_=pr[:], func=AF.Identity, bias=stt["nmr"][:, 0:1], scale=stt["rstd"][:, 0:1]),
                     reads=[sb_, prb], writes=[prb])
                S.op("dve", lambda e, pr=pr: e.tensor_tensor(out=pr[:], in0=pr[:], in1=gB[:], op=ALU.mult), reads=[prb, cb], writes=[prb])
                S.op("pool", lambda e, pr=pr: e.tensor_tensor(out=pr[:], in0=pr[:], in1=bB[:], op=ALU.add), reads=[prb, cb], writes=[prb])
                S.dma("sp", io["out"][rs, :], pr[:], reads=[prb])


def make_in_maps(cfg, inp):
    D, SEQ, NT, CTX = cfg.D, cfg.SEQ, cfg.NT, cfg.CTX
    f = lambda a: np.ascontiguousarray(np.asarray(a, dtype=np.float32))
    x, c, ctx, c_ctx = f(inp["x"]), f(inp["c"]), f(inp["ctx"]), f(inp["c_ctx"])
    w_in = f(inp["w_in"])[0]
    lbl = f(inp["lb_logits"])
    wzf = np.ascontiguousarray(w_in[:, cfg.off["zf"]:cfg.off["zf"] + cfg.AW])
    wzb = np.ascontiguousarray(w_in[:, cfg.off["zb"]:cfg.off["zb"] + cfg.AW])
    w_s = f(inp["w_s"])[0]
    b_s = f(inp["b_s"])[0]
    w_sT = np.ascontiguousarray(w_s.transpose(0, 2, 1))
    w_sT_rev = np.ascontiguousarray(w_sT[:, ::-1, ::-1])
    b_s_rev = np.ascontiguousarray(b_s[:, ::-1])
    tri = np.arange(128)
    consts = np.stack([np.eye(128, dtype=np.float32), (tri[:, None] <= tri[None, :]).astype(np.float32),
                       (tri[:, None] >= tri[None, :]).astype(np.float32)])
    shared = {k: f(inp[k])[0] for k in ("w_ada", "b_ada", "w_proj_a", "v_norm_g", "v_norm_b", "w_proj_b", "w_out",
                                        "ln1_g", "ln1_b", "w_ff1", "w_ff2", "ln2_g", "ln2_b")}
    shared["a_norm_g"] = f(inp["a_norm_g"])[0]
    shared["w_in"] = w_in
    shared["consts"] = consts
    maps = []
    for core in range(8):
        b, half = core // 2, core % 2
        rev = half == 1
        m = dict(shared)
        gidx = np.arange(SEQ)[::-1] if rev else np.arange(SEQ)
        m["xloc"] = np.ascontiguousarray(x[b][gidx])
        m["ctxl"] = np.ascontiguousarray(ctx[b][::-1] if rev else ctx[b])
        m["cvec"] = np.stack([c[b], c_ctx])
        m["w_zP"], m["w_zR"] = (wzb, wzf) if rev else (wzf, wzb)
        m["lblP"], m["lblR"] = (lbl[1, 0:2], lbl[0, 0:2]) if rev else (lbl[0, 0:2], lbl[1, 0:2])
        m["lblP"], m["lblR"] = np.ascontiguousarray(m["lblP"]), np.ascontiguousarray(m["lblR"])
        m["w_sT"] = w_sT_rev if rev else w_sT
        m["b_s"] = b_s_rev if rev else b_s
        rows, cols = gidx // GRID_W, gidx % GRID_W
        rs = np.zeros((SEQ // 128, 64, 128), np.float32)
        tl = np.arange(SEQ)
        rs[tl // 128, rows % 64 if cfg.ROWS > 64 else rows, tl % 128] = 1.0
        m["rowsel"] = rs
        cs = np.zeros((64, 128), np.float32)
        cs[cols[:128], np.arange(128)] = 1.0
        m["colsel"] = cs
        maps.append(m)
    return maps


_CACHE = {}


def run_cfg(cfg, inp, debug_outs=(), trace=False):
    key = (cfg.D, cfg.SEQ, tuple(debug_outs))
    if key not in _CACHE:
        _CACHE[key] = Builder(cfg, debug_outs).build()
    nc = _CACHE[key]
    maps = make_in_maps(cfg, inp)
    res = run_bass_kernel_spmd(nc, maps, core_ids=list(range(8)), **({"trace": True} if trace else {}))
    out = np.zeros((cfg.BATCH, cfg.SEQ, cfg.D), np.float32)
    for core in range(8):
        b, half = core // 2, core % 2
        o = np.asarray(res.results[core]["out"], dtype=np.float32)
        if half == 0:
            out[b, :cfg.NT] = o
        else:
            out[b, cfg.NT:] = o[::-1]
    return out, res


def kernel(**inputs):
    out, _ = run_cfg(Cfg(), inputs)
    return out
```
